# Optimizing a Trainium2 kernel written in Bass

```python
import math, functools
import jax, jax.numpy as jnp
from jax import lax
import numpy as np

D_MODEL = 2048
BATCH = 32
SEQ = 256
DEPTH = 2
DEC_BATCH = 2
DEC_SEQ = 1024
PAST_LEN = 512

GRID_W = 64
MIX_W = D_MODEL
ATT_W = MIX_W // 2
LRU_W = MIX_W // 4
HY_W = MIX_W // 4
HEAD_DIM = 64
N_HEADS = ATT_W // (2 * HEAD_DIM)
LRU_BLOCKS = 8
LRU_BW = LRU_W // LRU_BLOCKS
LRU_C = 8.0
LRU_CONV = 4
HY_CONV = 3
HY_BANDS = 16
HY_POS = 1 + 2 * HY_BANDS
HY_HIDDEN = 64
HY_DECAY_FAST = 0.3
HY_DECAY_SLOW = 1.5
HY_DECAY_TARGET = 1e-2
ROPE_BASE = 10000.0
Q_BLOCK = 128
EPS = 1e-6
IN_W = 4 * ATT_W + 2 * LRU_W + 4 * HY_W
SPLITS = [ATT_W, 2 * ATT_W, 3 * ATT_W, 4 * ATT_W, 4 * ATT_W + LRU_W, 4 * ATT_W + 2 * LRU_W, 4 * ATT_W + 2 * LRU_W + HY_W, 4 * ATT_W + 2 * LRU_W + 2 * HY_W, 4 * ATT_W + 2 * LRU_W + 3 * HY_W]

kernel_name = 'hymba_diffattn_rglru_hyena_diffusion_step'


def rmsnorm(x, g):
    xf = x.astype(jnp.float32)
    xf = xf * lax.rsqrt(jnp.mean(xf * xf, axis=-1, keepdims=True) + EPS)
    return (xf * g.astype(jnp.float32)).astype(x.dtype)


def dwconv(x, w, b, left, right):
    L = x.shape[1]
    xp = jnp.pad(x, ((0, 0), (left, right), (0, 0)))
    out = xp[:, 0:L] * w[0]
    for t in range(1, w.shape[0]):
        out = out + xp[:, t:t + L] * w[t]
    return out + b


def axial_rope(n_tok):
    rows = n_tok // GRID_W
    r, col = jnp.meshgrid(jnp.arange(rows, dtype=jnp.float32), jnp.arange(GRID_W, dtype=jnp.float32), indexing='ij')
    r = r.reshape(-1)
    col = col.reshape(-1)
    nf = HEAD_DIM // 4
    inv = ROPE_BASE ** (-jnp.arange(nf, dtype=jnp.float32) / nf)
    ang = jnp.stack([r[:, None] * inv, col[:, None] * inv], axis=1)
    return jnp.cos(ang), jnp.sin(ang)


def apply_axial_rope(x, cos, sin):
    nf = HEAD_DIM // 4
    xr = x.reshape(x.shape[:-1] + (2, HEAD_DIM // 2))
    a, b = xr[..., :nf], xr[..., nf:]
    cs = cos[None, :, None, None]
    sn = sin[None, :, None, None]
    out = jnp.concatenate([a * cs - b * sn, b * cs + a * sn], axis=-1)
    return out.reshape(x.shape).astype(x.dtype)


def diff_attention(q, k, v, lam):
    B, Lq = q.shape[0], q.shape[1]
    nblk = Lq // Q_BLOCK
    qb = jnp.moveaxis(q.reshape(B, nblk, Q_BLOCK, N_HEADS, 2, HEAD_DIM), 1, 0)
    scale = HEAD_DIM ** -0.5

    def block(qblk):
        s = jnp.einsum('bqhcd,bkhcd->bchqk', qblk, k, preferred_element_type=jnp.float32) * scale
        p = jax.nn.softmax(s, axis=-1)
        w = p[:, 0] - lam * p[:, 1]
        return jnp.einsum('bhqk,bkhe->bqhe', w.astype(v.dtype), v)

    out = lax.map(block, qb)
    return jnp.moveaxis(out, 0, 1).reshape(B, Lq, N_HEADS, 2 * HEAD_DIM)


def _combine(left, right):
    a_l, b_l = left
    a_r, b_r = right
    return a_l * a_r, a_r * b_l + b_r


def rglru_dir(xc, wa, ba, wi, bi, lam, h0, reverse):
    B, L, W = xc.shape
    xb = xc.reshape(B, L, LRU_BLOCKS, LRU_BW)
    r = jax.nn.sigmoid(jnp.einsum('blnc,ncd->blnd', xb, wa.astype(jnp.float32)).reshape(B, L, W) + ba.astype(jnp.float32))
    i = jax.nn.sigmoid(jnp.einsum('blnc,ncd->blnd', xb, wi.astype(jnp.float32)).reshape(B, L, W) + bi.astype(jnp.float32))
    log_a = -LRU_C * r * jax.nn.softplus(-lam.astype(jnp.float32))
    a = jnp.exp(log_a)
    b = jnp.sqrt(-jnp.expm1(2.0 * log_a)) * (i * xc)
    if reverse:
        a = jnp.flip(a, axis=1)
        b = jnp.flip(b, axis=1)
    b = b.at[:, 0].add(a[:, 0] * h0.astype(jnp.float32))
    _, h = lax.associative_scan(_combine, (a, b), axis=1)
    if reverse:
        h = jnp.flip(h, axis=1)
    return h


def hyena_filters(L, w1, b1, w2, b2, w3):
    pos = jnp.arange(L, dtype=jnp.float32)
    t = pos / float(max(L - 1, 1))
    bands = jnp.linspace(1e-4, HY_BANDS - 1, HY_BANDS, dtype=jnp.float32)
    ang = (2.0 * math.pi / L) * pos[:, None] * bands[None, :]
    z = jnp.concatenate([t[:, None], jnp.cos(ang), jnp.sin(ang)], axis=-1)
    hdn = jnp.sin(jnp.matmul(z, w1.astype(jnp.float32)) + b1.astype(jnp.float32))
    hdn = jnp.sin(jnp.matmul(hdn, w2.astype(jnp.float32)) + b2.astype(jnp.float32))
    filt = jnp.matmul(hdn, w3.astype(jnp.float32)).reshape(L, 2, HY_W)
    lo = abs(math.log(HY_DECAY_TARGET) / HY_DECAY_SLOW)
    hi = abs(math.log(HY_DECAY_TARGET) / HY_DECAY_FAST)
    deltas = jnp.linspace(lo, hi, HY_W, dtype=jnp.float32)
    decay = jnp.exp(-t[:, None] * deltas[None, :])
    filt = filt * decay[:, None, :]
    return filt[:, 0], filt[:, 1]


def bidir_long_conv(u, hf, hb, d):
    L, C = u.shape[1], u.shape[2]
    n = 2 * L
    g = jnp.concatenate([hf, jnp.zeros((1, C), jnp.float32), jnp.flip(hb[1:], axis=0)], axis=0)
    uf = u.astype(jnp.float32)
    y = jnp.fft.irfft(jnp.fft.rfft(uf, n=n, axis=1) * jnp.fft.rfft(g, n=n, axis=0)[None], n=n, axis=1)[:, :L]
    return (y + uf * d.astype(jnp.float32)).astype(u.dtype)


def run_layer(x, cond, rope, ctx_k, ctx_v, h0, lam, lam_init, norm_g, w_ada, b_ada, w_in, w_out, subln_g, lru_conv_w, lru_conv_b, lru_wa, lru_ba, lru_wi, lru_bi, lru_lam, hy_conv_w, hy_conv_b, hy_w1, hy_b1, hy_w2, hy_b2, hy_w3, hy_d):
    B, L, _ = x.shape
    mod = jnp.matmul(jax.nn.silu(cond), w_ada) + b_ada
    shift, scale, gate = jnp.split(mod[:, None, :], 3, axis=-1)
    h = rmsnorm(x, norm_g) * (1 + scale) + shift
    proj = jnp.matmul(h, w_in)
    q, k, v, ag, lx, lg, hv, hx1, hx0, hg = jnp.split(proj, SPLITS, axis=-1)

    q = q.reshape(B, L, N_HEADS, 2, HEAD_DIM)
    k = k.reshape(B, L, N_HEADS, 2, HEAD_DIM)
    v = v.reshape(B, L, N_HEADS, 2 * HEAD_DIM)
    k_store = k.reshape(B, L, N_HEADS, 2 * HEAD_DIM)
    v_store = v
    if rope is not None:
        q = apply_axial_rope(q, rope[0], rope[1])
        k = apply_axial_rope(k, rope[0], rope[1])
    if ctx_k is not None:
        P = ctx_k.shape[1]
        k = jnp.concatenate([ctx_k.reshape(B, P, N_HEADS, 2, HEAD_DIM).astype(k.dtype), k], axis=1)
        v = jnp.concatenate([ctx_v.astype(v.dtype), v], axis=1)
    att = diff_attention(q, k, v, lam)
    att = rmsnorm(att, subln_g) * (1.0 - lam_init)
    att = att.reshape(B, L, ATT_W) * jax.nn.silu(ag)

    xc = dwconv(lx, lru_conv_w, lru_conv_b, 2, 1).astype(jnp.float32)
    if h0 is None:
        h0 = jnp.zeros((B, 2, LRU_W), jnp.float32)
    h_f = rglru_dir(xc, lru_wa[0], lru_ba[0], lru_wi[0], lru_bi[0], lru_lam[0], h0[:, 0], False)
    h_b = rglru_dir(xc, lru_wa[1], lru_ba[1], lru_wi[1], lru_bi[1], lru_lam[1], h0[:, 1], True)
    h_last = jnp.stack([h_f[:, -1], h_b[:, 0]], axis=1)
    lru = (h_f + h_b).astype(x.dtype) * jax.nn.silu(lg)

    u = dwconv(jnp.concatenate([hv, hx1, hx0], axis=-1), hy_conv_w, hy_conv_b, 1, 1)
    hv, hx1, hx0 = jnp.split(u, 3, axis=-1)
    filt_f, filt_b = hyena_filters(L, hy_w1, hy_b1, hy_w2, hy_b2, hy_w3)
    hy = hx0 * bidir_long_conv(hx1 * hv, filt_f, filt_b, hy_d)
    hy = hy * jax.nn.silu(hg)

    out = jnp.matmul(jnp.concatenate([att, lru, hy], axis=-1), w_out)
    return x + gate * out, k_store, v_store, h_last


def setup_inputs(seed: int = 0) -> dict:
    key = jax.random.key(seed)
    ks = jax.random.split(key, 40)

    def nrm(k, shape, s):
        return s * jax.random.normal(k, shape, jnp.float32)

    u = jax.random.uniform(ks[23], (DEPTH, 2, LRU_W), jnp.float32, 0.9, 0.999)
    a = u ** (1.0 / LRU_C)
    return {
        'x_prompt': nrm(ks[0], (BATCH, SEQ, D_MODEL), 1.0),
        'x_sample': nrm(ks[1], (DEC_BATCH, DEC_SEQ, D_MODEL), 1.0),
        'cache_k': nrm(ks[2], (DEC_BATCH, DEPTH, PAST_LEN, N_HEADS, 2 * HEAD_DIM), 1.0),
        'cache_v': nrm(ks[3], (DEC_BATCH, DEPTH, PAST_LEN, N_HEADS, 2 * HEAD_DIM), 1.0),
        'state_lru': nrm(ks[4], (DEC_BATCH, DEPTH, 2, LRU_W), 0.5),
        'c': nrm(ks[5], (DEC_BATCH, D_MODEL), 1.0),
        'c_ctx': nrm(ks[6], (D_MODEL,), 1.0),
        'norm_g': 1.0 + nrm(ks[7], (DEPTH, D_MODEL), 0.02),
        'w_ada': nrm(ks[8], (DEPTH, D_MODEL, 3 * D_MODEL), 0.5 * D_MODEL ** -0.5),
        'b_ada': nrm(ks[9], (DEPTH, 3 * D_MODEL), 0.02),
        'w_in': nrm(ks[10], (DEPTH, D_MODEL, IN_W), D_MODEL ** -0.5),
        'w_out': nrm(ks[11], (DEPTH, MIX_W, D_MODEL), MIX_W ** -0.5),
        'lam_q1': nrm(ks[12], (DEPTH, HEAD_DIM), 0.1),
        'lam_k1': nrm(ks[13], (DEPTH, HEAD_DIM), 0.1),
        'lam_q2': nrm(ks[14], (DEPTH, HEAD_DIM), 0.1),
        'lam_k2': nrm(ks[15], (DEPTH, HEAD_DIM), 0.1),
        'attn_subln_g': 1.0 + nrm(ks[16], (DEPTH, 2 * HEAD_DIM), 0.02),
        'lru_conv_w': nrm(ks[17], (DEPTH, LRU_CONV, LRU_W), LRU_CONV ** -0.5),
        'lru_conv_b': nrm(ks[18], (DEPTH, LRU_W), 0.02),
        'lru_wa': nrm(ks[19], (DEPTH, 2, LRU_BLOCKS, LRU_BW, LRU_BW), LRU_BW ** -0.5),
        'lru_ba': nrm(ks[20], (DEPTH, 2, LRU_W), 0.02),
        'lru_wi': nrm(ks[21], (DEPTH, 2, LRU_BLOCKS, LRU_BW, LRU_BW), LRU_BW ** -0.5),
        'lru_bi': nrm(ks[22], (DEPTH, 2, LRU_W), 0.02),
        'lru_lam': jnp.log(a) - jnp.log1p(-a),
        'hy_conv_w': nrm(ks[24], (DEPTH, HY_CONV, 3 * HY_W), HY_CONV ** -0.5),
        'hy_conv_b': nrm(ks[25], (DEPTH, 3 * HY_W), 0.02),
        'hy_w1': nrm(ks[26], (DEPTH, HY_POS, HY_HIDDEN), HY_POS ** -0.5),
        'hy_b1': nrm(ks[27], (DEPTH, HY_HIDDEN), 0.1),
        'hy_w2': nrm(ks[28], (DEPTH, HY_HIDDEN, HY_HIDDEN), HY_HIDDEN ** -0.5),
        'hy_b2': nrm(ks[29], (DEPTH, HY_HIDDEN), 0.1),
        'hy_w3': nrm(ks[30], (DEPTH, HY_HIDDEN, 2 * HY_W), 0.05 * HY_HIDDEN ** -0.5),
        'hy_d': nrm(ks[31], (DEPTH, HY_W), 0.5),
        'final_g': 1.0 + nrm(ks[32], (D_MODEL,), 0.02),
    }


def reference(x_prompt, x_sample, cache_k, cache_v, state_lru, c, c_ctx, norm_g, w_ada, b_ada, w_in, w_out, lam_q1, lam_k1, lam_q2, lam_k2, attn_subln_g, lru_conv_w, lru_conv_b, lru_wa, lru_ba, lru_wi, lru_bi, lru_lam, hy_conv_w, hy_conv_b, hy_w1, hy_b1, hy_w2, hy_b2, hy_w3, hy_d, final_g):
    rope = axial_rope(x_sample.shape[1])
    ctx_cond = c_ctx[None, :]
    xp, xs = x_prompt, x_sample
    ks, vs, hs = [], [], []
    for l in range(DEPTH):
        lam_init = 0.8 - 0.6 * math.exp(-0.3 * l)
        lam = (jnp.exp(jnp.sum(lam_q1[l].astype(jnp.float32) * lam_k1[l].astype(jnp.float32)))
               - jnp.exp(jnp.sum(lam_q2[l].astype(jnp.float32) * lam_k2[l].astype(jnp.float32))) + lam_init)
        layer = functools.partial(
            run_layer, lam=lam, lam_init=lam_init, norm_g=norm_g[l], w_ada=w_ada[l], b_ada=b_ada[l],
            w_in=w_in[l], w_out=w_out[l], subln_g=attn_subln_g[l], lru_conv_w=lru_conv_w[l],
            lru_conv_b=lru_conv_b[l], lru_wa=lru_wa[l], lru_ba=lru_ba[l], lru_wi=lru_wi[l],
            lru_bi=lru_bi[l], lru_lam=lru_lam[l], hy_conv_w=hy_conv_w[l], hy_conv_b=hy_conv_b[l],
            hy_w1=hy_w1[l], hy_b1=hy_b1[l], hy_w2=hy_w2[l], hy_b2=hy_b2[l], hy_w3=hy_w3[l], hy_d=hy_d[l])
        xp, k_l, v_l, h_l = layer(xp, ctx_cond, None, None, None, None)
        ks.append(k_l)
        vs.append(v_l)
        hs.append(h_l)
        xs, _, _, _ = layer(xs, c, rope, cache_k[:, l], cache_v[:, l], state_lru[:, l])
    y_prompt = rmsnorm(xp, final_g)
    y_sample = rmsnorm(xs, final_g)
    new_cache_k = jnp.stack(ks, axis=1)
    new_cache_v = jnp.stack(vs, axis=1)
    new_state_lru = jnp.stack(hs, axis=1)
    return (y_prompt, y_sample, new_cache_k, new_cache_v, new_state_lru)
```

```python
import math
import numpy as np
import ml_dtypes
import concourse.bass as bass
import concourse.mybir as mybir
from concourse.bass_utils import run_bass_kernel_spmd

F32 = mybir.dt.float32
BF16 = mybir.dt.bfloat16
AF = mybir.ActivationFunctionType
ALU = mybir.AluOpType

D = 2048
NCORES = 8
DEPTH = 2
LP = 256
LS = 1024
PAST = 512
NH = 8
INW = 7168
EPS = 1e-6
TOK = 1024
PI = math.pi

R_LCW, R_LCB, R_LBA, R_LBI, R_LLAM, R_HCW, R_HCB, R_ST, R_SUB, R_B1, R_B2 = 0, 16, 20, 28, 36, 44, 80, 92, 100, 101, 102


def _dft_tables(L):
    n = 2 * L
    t = np.arange(L, dtype=np.float64)
    k = np.arange(L, dtype=np.float64)
    ang = 2.0 * np.pi * (k[None, :] + 0.5) * t[:, None] / n
    C = np.cos(ang)
    S = np.sin(ang)
    npiece = (2 * L) // 512
    cols = []
    for p in range(npiece):
        ks = slice(256 * p, 256 * p + 256)
        cols.append(C[:, ks])
        cols.append(S[:, ks])
    Fm = np.concatenate(cols, axis=1)
    Finv = (2.0 / n) * Fm.T
    return Fm.astype(ml_dtypes.bfloat16), Finv.astype(ml_dtypes.bfloat16)


def _hy_consts(L):
    pos = np.arange(L, dtype=np.float32)
    t = pos / np.float32(max(L - 1, 1))
    bands = np.linspace(1e-4, 16 - 1, 16, dtype=np.float32)
    ang = (np.float32(2.0 * math.pi / L) * pos[:, None] * bands[None, :]).astype(np.float32)
    z = np.concatenate([t[:, None], np.cos(ang), np.sin(ang)], axis=-1).astype(np.float32)
    lo = abs(math.log(1e-2) / 1.5)
    hi = abs(math.log(1e-2) / 0.3)
    deltas = np.linspace(lo, hi, 512, dtype=np.float32)
    decay = np.exp(-t[:, None] * deltas[None, :]).astype(np.float32)
    return np.ascontiguousarray(z.T), decay


def _rope_tables():
    nf = 16
    inv = (10000.0 ** (-np.arange(nf, dtype=np.float32) / nf)).astype(np.float32)
    tok = np.arange(LS)
    r = (tok // 64).astype(np.float32)
    c = (tok % 64).astype(np.float32)
    cosT = np.zeros((128, LS), np.float32)
    sinT = np.zeros((128, LS), np.float32)
    for p in range(128):
        half = (p % 64) // 32
        i = p % 16
        posv = r if half == 0 else c
        a = (posv * inv[i]).astype(np.float32)
        cosT[p] = np.cos(a)
        sinT[p] = np.sin(a)
    RT = np.zeros((128, 128), np.float32)
    for blk in range(4):
        for i in range(16):
            a_i = blk * 32 + i
            b_i = blk * 32 + 16 + i
            RT[b_i, a_i] = -1.0
            RT[a_i, b_i] = 1.0
    return cosT.astype(ml_dtypes.bfloat16), sinT.astype(ml_dtypes.bfloat16), RT.astype(ml_dtypes.bfloat16)


class Sem:
    def __init__(self, nc, name):
        self.h = nc.alloc_semaphore(name)
        self.count = 0


class Eng:
    def __init__(self, nc, name, e):
        self.name = name
        self.e = e
        self.sem = Sem(nc, "es_" + name)
        self.waited = {}


class Buf:
    __slots__ = ("name", "w", "r", "wsem", "rsem", "wsem_sw")

    def __init__(self, name):
        self.name = name
        self.w = {}
        self.r = {}
        self.wsem = None
        self.rsem = None
        self.wsem_sw = None


STRICT = True
SOFT = True


class StopBuild(Exception):
    pass


class K:
    def __init__(self, nc):
        self.nc = nc
        self.PE = Eng(nc, "pe", nc.tensor)
        self.ACT = Eng(nc, "act", nc.scalar)
        self.DVE = Eng(nc, "dve", nc.vector)
        self.POOL = Eng(nc, "pool", nc.gpsimd)
        self.SP = Eng(nc, "sp", nc.sync)
        self.engs = [self.PE, self.ACT, self.DVE, self.POOL, self.SP]
        self.sems = [e.sem for e in self.engs]
        self.nbuf = 0
        self.free_sems = []
        self.phase_sems = []
        self.carry = {}
        self.scope_bufs = []

    def get_sem(self, name):
        if self.free_sems:
            sm = self.free_sems.pop()
        else:
            sm = Sem(self.nc, name)
            self.sems.append(sm)
        self.phase_sems.append(sm)
        return sm

    def end_phase(self):
        self.free_sems.extend(self.phase_sems)
        self.phase_sems = []

    def keep(self, buf):
        for sm in (buf.wsem, buf.rsem):
            if sm is not None and sm in self.phase_sems:
                self.phase_sems.remove(sm)

    def buf(self, name="b"):
        self.nbuf += 1
        b = Buf(f"{name}{self.nbuf}")
        b.r = dict(self.carry)
        self.scope_bufs.append(b)
        return b

    def soft_barrier(self):
        if not SOFT:
            return self.barrier()
        for b in self.scope_bufs:
            self._acc(self.carry, b.w)
            self._acc(self.carry, b.r)
        self.scope_bufs = []

    def merge_into(self, dst_bufs, src_bufs):
        for d_ in dst_bufs:
            for s_ in src_bufs:
                self._acc(d_.r, s_.w)
                self._acc(d_.r, s_.r)

    def _wait(self, E, deps):
        for sem, val in deps.items():
            if E.waited.get(sem, 0) < val:
                E.e.wait_ge(sem.h, val)
                E.waited[sem] = val

    @staticmethod
    def _acc(deps, d, skip=None):
        for sem, val in d.items():
            if sem is skip:
                continue
            if deps.get(sem, 0) < val:
                deps[sem] = val

    def op(self, E, fn, reads=(), writes=(), inc=True):
        deps = {}
        for b in reads:
            self._acc(deps, b.w)
        for b in writes:
            self._acc(deps, b.w, skip=(None if STRICT else E.sem))
            self._acc(deps, b.r, skip=(None if STRICT else E.sem))
        if deps.get(E.sem, 0) > E.sem.count:
            deps[E.sem] = E.sem.count
        self._wait(E, deps)
        ins = fn()
        if inc:
            E.sem.count += 1
            ins.then_inc(E.sem.h, 1)
            val = E.sem.count
        else:
            val = E.sem.count + 1
        for b in reads:
            if b.r.get(E.sem, 0) < val:
                b.r[E.sem] = val
        for b in writes:
            b.w = {E.sem: val}
            b.r = {}
        return ins

    def dma_in(self, Q, buf, out_ap, in_ap, persist=False):
        deps = {}
        self._acc(deps, buf.w)
        self._acc(deps, buf.r)
        self._wait(Q, deps)
        if Q is self.POOL:
            if buf.wsem_sw is None:
                buf.wsem_sw = Sem(self.nc, "dsw_" + buf.name)
                self.sems.append(buf.wsem_sw)
            sm = buf.wsem_sw
        else:
            if buf.wsem is None:
                buf.wsem = self.get_sem("dw_" + buf.name)
                if persist:
                    self.phase_sems.remove(buf.wsem)
            sm = buf.wsem
        sm.count += 16
        Q.e.dma_start(out=out_ap, in_=in_ap).then_inc(sm.h, 16)
        buf.w = {sm: sm.count}
        buf.r = {}

    def dma_in_more(self, Q, buf, out_ap, in_ap):
        buf.wsem.count += 16
        Q.e.dma_start(out=out_ap, in_=in_ap).then_inc(buf.wsem.h, 16)
        buf.w = {buf.wsem: buf.wsem.count}

    def dma_out(self, Q, buf, out_ap, in_ap):
        deps = {}
        self._acc(deps, buf.w)
        self._wait(Q, deps)
        if buf.rsem is None:
            buf.rsem = self.get_sem("dr_" + buf.name)
        buf.rsem.count += 16
        Q.e.dma_start(out=out_ap, in_=in_ap).then_inc(buf.rsem.h, 16)
        buf.r[buf.rsem] = buf.rsem.count

    def barrier(self, engs=None, recycle=True):
        if engs is None and recycle:
            self._barrier(None)
            self.end_phase()
            self.carry = {}
            self.scope_bufs = []
        else:
            self._barrier(engs)

    def _barrier(self, engs=None):
        for E in (engs or self.engs):
            for s in self.sems:
                if s is E.sem:
                    continue
                if E.waited.get(s, 0) < s.count:
                    E.e.wait_ge(s.h, s.count)
                    E.waited[s] = s.count


def build_program(debug=None):
    nc = bass.Bass("TRN2", target_bir_lowering=False)
    k = K(nc)
    PE, ACT, DVE, POOL, SP = k.PE, k.ACT, k.DVE, k.POOL, k.SP

    def din(name, shape, dt=F32):
        return nc.dram_tensor(name, list(shape), dt, kind="ExternalInput").ap()

    def dout(name, shape, dt=F32):
        return nc.dram_tensor(name, list(shape), dt, kind="ExternalOutput").ap()

    xin = [din("xp", [TOK, D]), din("xs", [TOK, D])]
    ck = din("ck", [DEPTH, PAST, 1024])
    cv = din("cv", [DEPTH, PAST, 1024])
    st_lru = din("st_lru", [DEPTH, 2, 512])
    cvec = din("cvec", [2, D])
    norm_g = din("norm_g", [DEPTH, D])
    w_ada = din("w_ada", [DEPTH, D, 3 * D])
    b_ada = din("b_ada", [DEPTH, 3 * D])
    w_in = din("w_in", [DEPTH, D, INW])
    w_out = din("w_out", [DEPTH, D, D])
    lamv = [din(n, [DEPTH, 64]) for n in ("lam_q1", "lam_k1", "lam_q2", "lam_k2")]
    subln = din("attn_subln_g", [DEPTH, 128])
    lru_conv_w = din("lru_conv_w", [DEPTH, 4, 512])
    lru_conv_b = din("lru_conv_b", [DEPTH, 512])
    lru_wa = din("lru_wa", [DEPTH, 2, 8, 64, 64])
    lru_ba = din("lru_ba", [DEPTH, 2, 512])
    lru_wi = din("lru_wi", [DEPTH, 2, 8, 64, 64])
    lru_bi = din("lru_bi", [DEPTH, 2, 512])
    lru_lam = din("lru_lam", [DEPTH, 2, 512])
    hy_conv_w = din("hy_conv_w", [DEPTH, 3, 1536])
    hy_conv_b = din("hy_conv_b", [DEPTH, 1536])
    hy_w1 = din("hy_w1", [DEPTH, 33, 64])
    hy_b1 = din("hy_b1", [DEPTH, 64])
    hy_w2 = din("hy_w2", [DEPTH, 64, 64])
    hy_b2 = din("hy_b2", [DEPTH, 64])
    hy_w3 = din("hy_w3", [DEPTH, 64, 1024])
    hy_d = din("hy_d", [DEPTH, 512])
    final_g = din("final_g", [D])
    c_idf = din("c_idf", [128, 128])
    c_idb = din("c_idb", [128, 128], BF16)
    c_rt = din("c_rt", [128, 128], BF16)
    c_cos = din("c_cos", [128, LS], BF16)
    c_sin = din("c_sin", [128, LS], BF16)
    c_z = [din("c_z0", [33, LP]), din("c_z1", [33, LS])]
    c_dec = [din("c_dec0", [LP, 512]), din("c_dec1", [LS, 512])]
    c_F = [din("c_F0", [LP, 2 * LP], BF16), din("c_F1", [LS, 2 * LS], BF16)]
    c_Fi = [din("c_Fi0", [2 * LP, LP], BF16), din("c_Fi1", [2 * LS, LS], BF16)]
    yout = [dout("yp", [TOK, D]), dout("ys", [TOK, D])]
    nk = dout("nk", [4, DEPTH, LP, 1024])
    nv = dout("nv", [4, DEPTH, LP, 1024])
    nst = dout("nst", [4, DEPTH, 2, 512])
    dbg_outs = {}
    gscr = [[nc.dram_tensor(f"gscr{g_}{l_}", [128, (2 * (LP if g_ == 0 else LS)) // 128, 512], BF16, kind="Internal").ap()
             for l_ in range(DEPTH)] for g_ in range(2)]

    def sb(name, shape, dt=F32):
        return nc.alloc_sbuf_tensor(name, list(shape), dt).ap()

    uid = [0]

    def un(name):
        uid[0] += 1
        return f"{name}_{uid[0]}"

    XT = sb("XT", [128, 16, TOK], F32)
    HT = sb("HT", [128, 16, TOK], BF16)
    MT = sb("MT", [128, 16, TOK], BF16)
    WS = [sb(f"WS{i}", [128, 16, 512], BF16) for i in range(2)]
    bXT = [k.buf("xt") for _ in range(16)]
    bHT = [k.buf("ht") for _ in range(16)]
    bMT = [k.buf("mt") for _ in range(16)]
    bWS = [k.buf("ws") for _ in range(2)]
    IDF = sb("IDF", [128, 128], F32)
    IDB = sb("IDB", [128, 128], BF16)
    ONES = sb("ONES", [128, 128], BF16)
    bCONST = k.buf("const")
    GVT = sb("GVT", [128, 80], F32)
    bGVT = k.buf("gvt")
    PVT = [sb(f"PVT{l}", [128, 104], F32) for l in range(DEPTH)]
    bPVT = [k.buf("pvt") for _ in range(DEPTH)]
    MODT = [sb(f"MODT{l}", [128, 48, 2], F32) for l in range(DEPTH)]
    bMODT = [k.buf("modt") for _ in range(DEPTH)]
    SMALL = sb("SMALL", [128, 64], F32)
    bSMALL = k.buf("small")
    LAMS = sb("LAMS", [128, 8], F32)
    bLAMS = k.buf("lams")
    PS = [nc.alloc_psum_tensor(f"PS{i}", [128, 512], F32).ap() for i in range(8)]
    bPS = [k.buf("ps") for _ in range(8)]
    ring = {"A": [0, 1, 2, 3], "B": [4, 5, 6, 7], "ALL": list(range(8)), "G": [4, 5, 6], "O": [0, 1, 2, 3, 7], "O3": [0, 1, 2]}
    rpos = {"A": 0, "B": 0, "ALL": 0, "G": 0, "O": 0, "O3": 0}

    def psum(rname="A"):
        lst = ring[rname]
        i = lst[rpos[rname] % len(lst)]
        rpos[rname] += 1
        return PS[i], bPS[i]

    def dump(name, ap, b, shape, dt=F32):
        if debug is None or name not in debug:
            return
        o = dout("dbg_" + name, shape, dt)
        dbg_outs[name] = o
        bb = b if isinstance(b, (list, tuple)) else [b]
        for x in bb[:-1]:
            deps = {}
            k._acc(deps, x.w)
            k._wait(SP, deps)
        k.dma_out(SP, bb[-1], o, ap)

    def mm(out, lhsT, rhs, start, stop, reads, writes, inc=None):
        if inc is None:
            inc = stop
        k.op(PE, lambda: nc.tensor.matmul(out, lhsT, rhs, start=start, stop=stop), reads, writes, inc)

    def tr(out, in_, ident, reads, writes, inc=True):
        k.op(PE, lambda: nc.tensor.transpose(out, in_, ident), reads, writes, inc)

    def act(out, in_, func, reads, writes, bias=None, scale=None, accum_out=None):
        kw = {}
        if bias is not None:
            kw["bias"] = bias
        if scale is not None:
            kw["scale"] = scale
        if accum_out is not None:
            kw["accum_out"] = accum_out
        k.op(ACT, lambda: nc.scalar.activation(out=out, in_=in_, func=func, **kw), reads, writes)

    def tt(E, out, a, b, op_, reads, writes):
        k.op(E, lambda: E.e.tensor_tensor(out, a, b, op_), reads, writes)

    def ts(E, out, a, s1, s2, op0, op1, reads, writes):
        if op1 is None:
            k.op(E, lambda: E.e.tensor_scalar(out, a, s1, None, op0), reads, writes)
        else:
            k.op(E, lambda: E.e.tensor_scalar(out, a, s1, s2, op0, op1), reads, writes)

    def stt(out, a, s, b, op0, op1, reads, writes):
        k.op(DVE, lambda: nc.vector.scalar_tensor_tensor(out, a, s, b, op0, op1), reads, writes)

    def cp(E, out, in_, reads, writes):
        if E is ACT:
            act(out, in_, AF.Copy, reads, writes)
        else:
            k.op(E, lambda: E.e.tensor_copy(out, in_), reads, writes)

    def wsrc_cols(w, l, c0, n=512):
        return w[l].rearrange("(kc p) n -> p kc n", p=128)[:, :, c0:c0 + n]

    items = []

    def add_item(label, src, shape, cast):
        items.append((label, src, shape, cast))

    for l in range(DEPTH):
        for t in range(12):
            add_item(("ada", l, t), wsrc_cols(w_ada, l, 512 * t), (16, 512), True)

    def pass_items(l, g):
        L = LP if g == 0 else LS
        gi = 0 if g == 0 else 1
        for blk in range(2):
            add_item(("q", l, g, blk), wsrc_cols(w_in, l, 0 + 512 * blk), (16, 512), True)
            add_item(("k", l, g, blk), wsrc_cols(w_in, l, 1024 + 512 * blk), (16, 512), True)
            add_item(("v", l, g, blk), wsrc_cols(w_in, l, 2048 + 512 * blk), (16, 512), True)
        for blk in range(2):
            add_item(("ag", l, g, blk), wsrc_cols(w_in, l, 3072 + 512 * blk), (16, 512), True)
        add_item(("lx", l, g), wsrc_cols(w_in, l, 4096), (16, 512), True)
        add_item(("lg", l, g), wsrc_cols(w_in, l, 4608), (16, 512), True)
        T = L // 128
        npiece = (2 * L) // 512
        Fv = c_F[gi].rearrange("(t p) f -> p t f", p=128)
        add_item(("hv", l, g), wsrc_cols(w_in, l, 5120), (16, 512), True)
        add_item(("hx1", l, g), wsrc_cols(w_in, l, 5632), (16, 512), True)
        for p in range(npiece):
            add_item(("Ff", l, g, p), Fv[:, :, 512 * p:512 * p + 512], (T, 512), False)
        Fiv = c_Fi[gi].rearrange("(kc p) t -> p kc t", p=128)
        if g == 0:
            add_item(("Fi", l, g, 0), Fiv, (4, 256), False)
        else:
            for h in range(2):
                add_item(("Fi", l, g, h), Fiv[:, :, 512 * h:512 * h + 512], (16, 512), False)
        add_item(("hx0", l, g), wsrc_cols(w_in, l, 6144), (16, 512), True)
        add_item(("hg", l, g), wsrc_cols(w_in, l, 6656), (16, 512), True)
        for t in range(4):
            add_item(("wo", l, g, t), wsrc_cols(w_out, l, 512 * t), (16, 512), True)

    for g in range(2):
        for l in range(DEPTH):
            pass_items(l, g)

    wstate = {"i": 0, "issued": 0}

    def w_issue(i):
        label, src, shape, cast = items[i]
        slot = i % 2
        dst = WS[slot][:, 0:shape[0], 0:shape[1]]
        Q = POOL if cast else SP
        k.dma_in(Q, bWS[slot], dst, src, persist=True)

    def wprefetch():
        i = wstate["i"]
        while wstate["issued"] <= min(i, len(items) - 1):
            w_issue(wstate["issued"])
            wstate["issued"] += 1

    def wnext(label, prefetch=True):
        i = wstate["i"]
        assert items[i][0] == label, (items[i][0], label)
        while wstate["issued"] <= min(i + (1 if prefetch else 0), len(items) - 1):
            w_issue(wstate["issued"])
            wstate["issued"] += 1
        wstate["i"] += 1
        shape = items[i][2]
        return WS[i % 2], bWS[i % 2]

    k.dma_in(SP, bCONST, IDF, c_idf, persist=True)
    k.dma_in_more(SP, bCONST, IDB, c_idb)
    k.op(DVE, lambda: nc.vector.memset(ONES, 1.0), [], [bCONST])

    with nc.sbuf_tensor(un("GV"), [128, 128], F32) as GVh, nc.sbuf_tensor(un("PV0"), [128, 128], F32) as PV0h, \
            nc.sbuf_tensor(un("PV1"), [128, 128], F32) as PV1h, nc.sbuf_tensor(un("LQ"), [128, 4, DEPTH, 64], F32) as LQh, \
            nc.sbuf_tensor(un("LT"), [128, DEPTH, 64], F32) as LTh:
        GV = GVh[:]
        PV = [PV0h[:], PV1h[:]]
        LQ = LQh[:]
        LT = LTh[:]
        bGV = k.buf("gv")
        k.op(DVE, lambda: nc.vector.memset(GV, 0.0), [], [bGV])
        k.dma_in(SP, bGV, GV[0:32, :], cvec.rearrange("r (kc p) -> (r kc) p", p=128))
        k.dma_in_more(SP, bGV, GV[32:64, :], norm_g.rearrange("l (kc p) -> (l kc) p", p=128))
        k.dma_in_more(SP, bGV, GV[64:80, :], final_g.rearrange("(kc p) -> kc p", p=128))
        ps, bps = psum()
        tr(ps[:, 0:128], GV, IDF, [bGV, bCONST], [bps])
        cp(DVE, GVT, ps[:, 0:80], [bps], [bGVT])
        for l in range(DEPTH):
            bPV = k.buf("pv")
            k.op(DVE, lambda: nc.vector.memset(PV[l], 0.0), [], [bPV])
            P_ = PV[l]
            k.dma_in(SP, bPV, P_[R_LCW:R_LCW + 16, :], lru_conv_w[l].rearrange("k (j p) -> (k j) p", p=128))
            k.dma_in_more(SP, bPV, P_[R_LCB:R_LCB + 4, :], lru_conv_b[l].rearrange("(j p) -> j p", p=128))
            k.dma_in_more(SP, bPV, P_[R_LBA:R_LBA + 8, :], lru_ba[l].rearrange("d (j p) -> (d j) p", p=128))
            k.dma_in_more(SP, bPV, P_[R_LBI:R_LBI + 8, :], lru_bi[l].rearrange("d (j p) -> (d j) p", p=128))
            k.dma_in_more(SP, bPV, P_[R_LLAM:R_LLAM + 8, :], lru_lam[l].rearrange("d (j p) -> (d j) p", p=128))
            k.dma_in_more(SP, bPV, P_[R_HCW:R_HCW + 36, :], hy_conv_w[l].rearrange("k (m p) -> (k m) p", p=128))
            k.dma_in_more(SP, bPV, P_[R_HCB:R_HCB + 12, :], hy_conv_b[l].rearrange("(m p) -> m p", p=128))
            k.dma_in_more(SP, bPV, P_[R_ST:R_ST + 8, :], st_lru[l].rearrange("d (j p) -> (d j) p", p=128))
            k.dma_in_more(SP, bPV, P_[R_SUB:R_SUB + 1, :], subln[l:l + 1, :])
            k.dma_in_more(SP, bPV, P_[R_B1:R_B1 + 1, 0:64], hy_b1[l:l + 1, :])
            k.dma_in_more(SP, bPV, P_[R_B2:R_B2 + 1, 0:64], hy_b2[l:l + 1, :])
            ps, bps = psum()
            tr(ps[:, 0:128], P_, IDF, [bPV, bCONST], [bps])
            cp(DVE, PVT[l], ps[:, 0:104], [bps], [bPVT[l]])
        bLQ = k.buf("lq")
        for i in range(4):
            for l in range(DEPTH):
                if i == 0 and l == 0:
                    k.dma_in(SP, bLQ, LQ[:, i, l, :], lamv[i][l].partition_broadcast(128))
                else:
                    k.dma_in_more(SP, bLQ, LQ[:, i, l, :], lamv[i][l].partition_broadcast(128))
        bLT = k.buf("lt")
        tt(DVE, LT, LQ[:, 0], LQ[:, 1], ALU.mult, [bLQ], [bLT])
        k.op(DVE, lambda: nc.vector.reduce_sum(LAMS[:, 0:2], LT, mybir.AxisListType.X), [bLT], [bLAMS])
        tt(DVE, LT, LQ[:, 2], LQ[:, 3], ALU.mult, [bLQ], [bLT])
        k.op(DVE, lambda: nc.vector.reduce_sum(LAMS[:, 2:4], LT, mybir.AxisListType.X), [bLT], [bLAMS])
        act(LAMS[:, 0:4], LAMS[:, 0:4], AF.Exp, [bLAMS], [bLAMS])
        tt(DVE, LAMS[:, 0:2], LAMS[:, 0:2], LAMS[:, 2:4], ALU.subtract, [bLAMS], [bLAMS])
        for l in range(DEPTH):
            lam_init = 0.8 - 0.6 * math.exp(-0.3 * l)
            ts(DVE, LAMS[:, 4 + l:5 + l], LAMS[:, l:l + 1], lam_init, -1.0, ALU.add, ALU.mult, [bLAMS], [bLAMS])
        k.barrier()

    def gen_G(g, l):
        with nc.sbuf_tensor(un("gZT"), [33, 512], F32) as ZTh, nc.sbuf_tensor(un("gHW1"), [33, 64], F32) as HW1h, \
                nc.sbuf_tensor(un("gHW2"), [64, 64], F32) as HW2h, nc.sbuf_tensor(un("gHW3"), [64, 1024], BF16) as HW3h, \
                nc.sbuf_tensor(un("gH1"), [64, 512], F32) as H1h, nc.sbuf_tensor(un("gH2"), [64, LS], BF16) as H2h, \
                nc.sbuf_tensor(un("gDEC"), [128, 2, 512], F32) as DECh, nc.sbuf_tensor(un("gDROW"), [1, 512], F32) as DRh, \
                nc.sbuf_tensor(un("gFT"), [128, 2, 512], F32) as FTh, \
                nc.sbuf_tensor(un("gGST"), [128, 2, 512], BF16) as GSTh:
            ZT, HW1, HW2, HW3, H1, H2, DEC, DROW, FT, GST = (ZTh[:], HW1h[:], HW2h[:], HW3h[:], H1h[:], H2h[:],
                                                            DECh[:], DRh[:], FTh[:], GSTh[:])
            FB = HT[:, 0:8, :].rearrange("p (a b) (c d) -> p a (b c) d", a=2, d=512)
            WP = HT[:, 8:12, :].rearrange("p a (b c) -> p (a b) c", c=512)
            WM = HT[:, 12:16, :].rearrange("p a (b c) -> p (a b) c", c=512)
            bF = k.buf("filt")
            bF3 = k.buf("filt3")
            bZT = k.buf("zt")
            bH1 = k.buf("h1")
            bH2 = k.buf("h2")
            bDEC = [k.buf("dec"), k.buf("dec")]
            bWPM = k.buf("wpm")
            bFT = [k.buf("ft"), k.buf("ft")]
            bFB = [k.buf("fb"), k.buf("fb")]
            bGST = [k.buf("gst"), k.buf("gst")]
            k.merge_into(bFB + [bWPM], bHT)
            gn = [0]
            fn = [0]
            if True:
                L = LP if g == 0 else LS
                T = L // 128
                npiece = (2 * L) // 512
                Fv = c_F[g].rearrange("(t p) f -> p t f", p=128)
                decv = c_dec[g].rearrange("(t p) c -> p t c", p=128)
                if True:
                    pv = PVT[l]
                    k.dma_in(SP, bF, HW1, hy_w1[l])
                    k.dma_in_more(SP, bF, HW2, hy_w2[l])
                    k.dma_in_more(SP, bF, DROW, hy_d[l:l + 1, :])
                    k.dma_in(POOL, bF3, HW3, hy_w3[l])
                    nch = max(1, L // 512)
                    cw_ = min(L, 512)
                    for ci in range(nch):
                        k.dma_in(SP, bZT, ZT[:, 0:cw_], c_z[g][:, cw_ * ci:cw_ * ci + cw_])
                        ps, bps = psum("G")
                        mm(ps[0:64, 0:cw_], HW1, ZT[:, 0:cw_], True, True, [bF, bZT], [bps])
                        wrap_sin(H1[:, 0:cw_], bH1, ps[0:64, 0:cw_], bps, pv[0:64, R_B1:R_B1 + 1],
                                 FT[0:64, 0, 0:cw_], FT[0:64, 1, 0:cw_], bFT[0], bFT[1], [bPVT[l]])
                        yield
                        ps, bps = psum("G")
                        mm(ps[0:64, 0:cw_], HW2, H1[:, 0:cw_], True, True, [bF, bH1], [bps])
                        wrap_sin(H2[:, cw_ * ci:cw_ * ci + cw_], bH2, ps[0:64, 0:cw_], bps, pv[0:64, R_B2:R_B2 + 1],
                                 FT[0:64, 0, 0:cw_], FT[0:64, 1, 0:cw_], bFT[0], bFT[1], [bPVT[l]])
                        yield
                    k.dma_in(SP, bDEC[0], DEC[:, 0, :], decv[:, 0, :])
                    for t in range(T):
                        r = t % 2
                        if t + 1 < T:
                            k.dma_in(SP, bDEC[1 - r], DEC[:, 1 - r, :], decv[:, t + 1, :])
                        psf, bpsf = psum("G")
                        psb_, bpsb = psum("G")
                        mm(psf, H2[:, 128 * t:128 * t + 128], HW3[:, 0:512], True, True, [bH2, bF3], [bpsf])
                        mm(psb_, H2[:, 128 * t:128 * t + 128], HW3[:, 512:1024], True, True, [bH2, bF3], [bpsb])
                        hf = FT[:, 0, :]
                        hb = FT[:, 1, :]
                        tt(DVE, hf, psf, DEC[:, r, :], ALU.mult, [bpsf, bDEC[r]], [bFT[0]])
                        tt(DVE, hb, psb_, DEC[:, r, :], ALU.mult, [bpsb, bDEC[r]], [bFT[1]])
                        if t == 0:
                            k.op(DVE, lambda: nc.vector.memset(hb[0:1, :], 0.0), [], [bFT[1]])
                            tt(DVE, hf[0:1, :], hf[0:1, :], DROW, ALU.add, [bFT[0], bF], [bFT[0]])
                        tt(DVE, WP[:, t, :], hf, hb, ALU.add, [bFT[0], bFT[1]], [bWPM])
                        tt(DVE, WM[:, t, :], hf, hb, ALU.subtract, [bFT[0], bFT[1]], [bWPM])
                        yield
                    k.dma_in(SP, bFB[fn[0] % 2], FB[:, fn[0] % 2, 0:T, :], Fv[:, :, 0:512])
                    for p in range(npiece):
                        fr = fn[0] % 2
                        fn[0] += 1
                        if p + 1 < npiece:
                            k.dma_in(SP, bFB[1 - fr], FB[:, 1 - fr, 0:T, :], Fv[:, :, 512 * (p + 1):512 * (p + 1) + 512])
                        for q in range(4):
                            kc = 4 * p + q
                            src = WP if q < 2 else WM
                            ps, bps = psum("G")
                            for t in range(T):
                                mm(ps, FB[:, fr, t, 128 * q:128 * q + 128], src[:, t, :], t == 0, t == T - 1,
                                   [bFB[fr], bWPM], [bps])
                            r = gn[0] % 2
                            gn[0] += 1
                            cp(ACT, GST[:, r, :], ps, [bps], [bGST[r]])
                            k.dma_out(SP, bGST[r], gscr[g][l][:, kc, :], GST[:, r, :])
                            yield
            k.merge_into(bHT, bFB + [bWPM])
            k.gen_bufs = [bF, bF3, bZT, bH1, bH2, bWPM] + bDEC + bFT + bFB + bGST

    def phase_ada():
        with nc.sbuf_tensor(un("SC"), [128, 16, 2], BF16) as SCh, nc.sbuf_tensor(un("MODROW"), [2, 2, 512], F32) as MRh, \
                nc.sbuf_tensor(un("BADA"), [2, 2, 512], F32) as BAh:
            SC = SCh[:]
            MR = MRh[:]
            BA = BAh[:]
            bSC = k.buf("sc")
            ggen = gen_G(0, 0)
            bMR = [k.buf("mr"), k.buf("mr")]
            bBA = [k.buf("ba"), k.buf("ba")]
            for r in range(2):
                act(SC[:, :, r], GVT[:, 16 * r:16 * r + 16], AF.Silu, [bGVT], [bSC])
            def first_pass_prologue():
                yield from load_x_gen(0, "G")
                yield from phase_norm_gen(0, 0, "G")

            fpp = None
            for l in range(DEPTH):
                psR, bpsR = PS[7], bPS[7]
                if l == 1:
                    for _ in ggen:
                        pass
                    fpp = first_pass_prologue()
                for t in range(12):
                    r = t % 2
                    k.dma_in(SP, bBA[r], BA[:, r, :], b_ada[l, 512 * t:512 * t + 512].partition_broadcast(2))
                    W, bW = wnext(("ada", l, t))
                    ps, bps = psum()
                    for kc in range(16):
                        mm(ps[0:2, :], SC[:, kc, :], W[:, kc, :], kc == 0, kc == 15, [bSC, bW], [bps])
                    tt(DVE, MR[:, r, :], ps[0:2, :], BA[:, r, :], ALU.add, [bps, bBA[r]], [bMR[r]])
                    for j4 in range(4):
                        j = 4 * t + j4
                        mm(psR[:, 2 * j:2 * j + 2], MR[0:2, r, 128 * j4:128 * j4 + 128], IDF[0:2, 0:2], True, True,
                           [bMR[r], bCONST], [bpsR], inc=True)
                    for _ in range(4):
                        next(ggen, None)
                    if fpp is not None:
                        for _ in range(2):
                            next(fpp, None)
                cp(DVE, MODT[l].rearrange("p a b -> p (a b)"), psR[:, 0:96], [bpsR], [bMODT[l]])
            for _ in ggen:
                pass
            if fpp is not None:
                for _ in fpp:
                    pass
            k.barrier()

    def load_x(g):
        for _ in load_x_gen(g):
            pass

    def load_x_gen(g, rname="ALL"):
        with nc.sbuf_tensor(un("XST0"), [128, D], F32) as s0, nc.sbuf_tensor(un("XST1"), [128, D], F32) as s1:
            st = [s0[:], s1[:]]
            bst = [k.buf("xst"), k.buf("xst")]
            for i in range(8):
                s = st[i % 2]
                bs = bst[i % 2]
                k.dma_in(SP, bs, s, xin[g][128 * i:128 * i + 128, :])
                for q4 in range(4):
                    ps, bps = psum(rname)
                    for j in range(4):
                        dc = 4 * q4 + j
                        tr(ps[:, 128 * j:128 * j + 128], s[:, 128 * dc:128 * dc + 128], IDF, [bs, bCONST], [bps],
                           inc=(j == 3))
                    E = ACT if q4 % 2 == 0 else DVE
                    cp(E, XT[:, 4 * q4:4 * q4 + 4, 128 * i:128 * i + 128],
                       ps.rearrange("p (a b) -> p a b", a=4), [bps], bXT[4 * q4:4 * q4 + 4])
                yield
            k.soft_barrier()

    def norm_stats(RB, bRB):
        for _ in norm_stats_gen(RB, bRB):
            pass

    def norm_stats_gen(RB, bRB, rname="B"):
        with nc.sbuf_tensor(un("SQ"), [128, 4, 512], BF16) as SQh:
            SQ = SQh[:]
            bSQ = [k.buf("sq") for _ in range(4)]
            pss = [psum(rname) for _ in range(2)]
            n = 0
            for dc in range(16):
                for tc in range(2):
                    r = n % 4
                    n += 1
                    act(SQ[:, r, :], XT[:, dc, 512 * tc:512 * tc + 512], AF.Square, [bXT[dc]], [bSQ[r]])
                    mm(pss[tc][0], ONES, SQ[:, r, :], dc == 0, dc == 15, [bSQ[r], bCONST], [pss[tc][1]], inc=True)
                if dc % 4 == 3:
                    yield
            for tc in range(2):
                act(RB[:, 512 * tc:512 * tc + 512], pss[tc][0], AF.Ln, [pss[tc][1]], [bRB], scale=1.0 / D, bias=EPS_AP)
            act(RB, RB, AF.Exp, [bRB], [bRB], scale=-0.5)
            k.soft_barrier()

    def phase_norm(l, g):
        for _ in phase_norm_gen(l, g):
            pass

    def phase_norm_gen(l, g, rname="B"):
        with nc.sbuf_tensor(un("RB"), [128, TOK], F32) as RBh, nc.sbuf_tensor(un("NT"), [128, 2, TOK], F32) as NTh:
            RB = RBh[:]
            NT = NTh[:]
            bRB = k.buf("rb")
            bNT = [k.buf("nt"), k.buf("nt")]
            yield from norm_stats_gen(RB, bRB, rname)
            stt(SMALL[:, 0:16], MODT[l][:, 16:32, g], 1.0, GVT[:, 32 + 16 * l:48 + 16 * l], ALU.add, ALU.mult,
                [bMODT[l], bGVT], [bSMALL])
            for dc in range(16):
                r = dc % 2
                stt(NT[:, r, :], XT[:, dc, :], SMALL[:, dc:dc + 1], RB, ALU.mult, ALU.mult,
                    [bXT[dc], bSMALL, bRB], [bNT[r]])
                act(HT[:, dc, :], NT[:, r, :], AF.Identity, [bNT[r], bMODT[l]], [bHT[dc]],
                    bias=MODT[l][:, dc, g:g + 1])
                if dc % 2 == 1:
                    yield
            k.soft_barrier()

    def proj_fm(W, bW, j, tc, ps, bps, ntok=512):
        for kc in range(16):
            mm(ps, W[:, kc, 128 * j:128 * j + 128], HT[:, kc, 512 * tc:512 * tc + 512], kc == 0, kc == 15,
               [bW, bHT[kc]], [bps])

    def proj_tm(W, bW, i, ps, bps):
        for kc in range(16):
            mm(ps, HT[:, kc, 128 * i:128 * i + 128], W[:, kc, :], kc == 0, kc == 15, [bW, bHT[kc]], [bps])

    def phase_attn(l, g):
        nseq, L = (4, LP) if g == 0 else (1, LS)
        nctx = 0 if g == 0 else PAST
        KL = L + nctx
        KT_TOT = nseq * KL
        nkt_seq = KL // 128
        nq = 256 if g == 0 else 512
        nqc = L // nq
        lam_init = 0.8 - 0.6 * math.exp(-0.3 * l)
        NKST = 4 if g == 0 else 2
        V = MT[:, 8:14, :].rearrange("p a (b c) -> p (a b) c", c=512)
        if debug and debug.get("realV"):
            V = sb(un("Vreal"), [128, 8, 512], BF16)
        ROPE = MT[:, 14:16, :]
        with nc.sbuf_tensor(un("QT"), [128, 4, TOK], BF16) as QTh, nc.sbuf_tensor(un("KTt"), [128, 4, KT_TOT], BF16) as KTh, \
                nc.sbuf_tensor(un("ET"), [128, 4, 512], BF16) as ETh, nc.sbuf_tensor(un("KST"), [128, NKST, 512], F32 if g == 0 else BF16) as KSTh, \
                nc.sbuf_tensor(un("RAW"), [128, 2, 512], BF16) as RAWh, \
                nc.sbuf_tensor(un("AT"), [128, 3, 512], F32) as ATh, nc.sbuf_tensor(un("ROPT"), [128, 2, 512], BF16) as ROPTh, \
                nc.sbuf_tensor(un("RTM"), [128, 128], BF16) as RTMh, nc.sbuf_tensor(un("KCX"), [128, 4, 512], BF16) as KCXh, \
                nc.sbuf_tensor(un("SQA"), [128, 512], BF16) as SQAh, \
                nc.sbuf_tensor(un("QM"), [128, 2, 512 if g == 1 else 2], BF16) as QMh:
            QT, KT, ET, KST, RAW, AT, ROPT, RTM, KCX, SQA, QM = (QTh[:], KTh[:], ETh[:], KSTh[:], RAWh[:], ATh[:], ROPTh[:],
                                                               RTMh[:], KCXh[:], SQAh[:], QMh[:])
            bQM = [k.buf("qm"), k.buf("qm")]
            bQT = [k.buf("qt") for _ in range(4)]
            bKT = [k.buf("kt") for _ in range(4)]
            bV = k.buf("v")
            bET = [k.buf("et") for _ in range(4)]
            bKST = [k.buf("kst") for _ in range(NKST)]
            bRAW = [k.buf("raw") for _ in range(2)]
            bAT = [k.buf("at") for _ in range(3)]
            itn = [0]
            en = [0]
            tail = [None]
            qm_ready = [False]
            prestart = [None]
            bROPT = [k.buf("ropt") for _ in range(2)]
            bROPE = k.buf("rope")
            k.merge_into([bV], bMT[8:14])
            k.merge_into([bROPE], bMT[14:16])
            bKCX = [k.buf("kcx") for _ in range(4)]
            bSQA = k.buf("sqa")
            ts(DVE, SMALL[:, 33:34], PVT[l][:, R_SUB:R_SUB + 1], 1.0 - lam_init, None, ALU.mult, None, [bPVT[l]], [bSMALL])
            if g == 1:
                k.dma_in(SP, bROPE, ROPE[:, 0, :], c_cos)
                k.dma_in_more(SP, bROPE, ROPE[:, 1, :], c_sin)
                k.dma_in_more(SP, bROPE, RTM, c_rt)

            def rope_apply(src, rd_src, dst, bdst, tok0):
                ps2, bps2 = psum("A")
                mm(ps2, RTM, src, True, True, [bROPE] + rd_src, [bps2])
                a0 = ROPT[:, 0, :]
                a1 = ROPT[:, 1, :]
                tt(DVE, a0, src, ROPE[:, 0, tok0:tok0 + 512], ALU.mult, rd_src + [bROPE], [bROPT[0]])
                tt(DVE, a1, ps2, ROPE[:, 1, tok0:tok0 + 512], ALU.mult, [bps2, bROPE], [bROPT[1]])
                tt(DVE, dst, a0, a1, ALU.add, [bROPT[0], bROPT[1]], [bdst])

            rawn = [0]
            for blk in range(2):
                if g == 1:
                    ckv = ck[l].rearrange("(t p) c -> p t c", p=128)
                    for t in range(4):
                        k.dma_in(POOL, bKCX[t], KCX[:, t, :], ckv[:, t, 512 * blk:512 * blk + 512])
                W, bW = wnext(("q", l, g, blk))
                for j in range(4):
                    for tc in range(2):
                        ps, bps = psum("A")
                        proj_fm(W, bW, j, tc, ps, bps)
                        if g == 0:
                            cp(ACT, QT[:, j, 512 * tc:512 * tc + 512], ps, [bps], [bQT[j]])
                        else:
                            r = rawn[0] % 2
                            rawn[0] += 1
                            cp(ACT, RAW[:, r, :], ps, [bps], [bRAW[r]])
                            rope_apply(RAW[:, r, :], [bRAW[r]], QT[:, j, 512 * tc:512 * tc + 512], bQT[j], 512 * tc)
                if debug and debug.get("sub") == "q":
                    k.barrier(recycle=False)
                    raise StopBuild()
                W, bW = wnext(("k", l, g, blk))
                if g == 1:
                    for t in range(4):
                        r = t
                        ps, bps = psum("A")
                        psb = ps.bitcast(BF16)
                        for j in range(4):
                            tr(psb[:, 128 * j:128 * j + 128], KCX[:, r, 128 * j:128 * j + 128], IDB, [bKCX[r], bCONST], [bps],
                               inc=(j == 3))
                        cp(DVE, KT[:, :, 128 * t:128 * t + 128], psb[:, 0:512].rearrange("p (a b) -> p a b", a=4),
                           [bps], bKT)
                pend_tr = [None]

                def k_transposes(i, r):
                    ps2, bps2 = psum("A")
                    if g == 0:
                        pv_, idn = ps2, IDF
                    else:
                        pv_, idn = ps2.bitcast(BF16), IDB
                    for j in range(4):
                        tr(pv_[:, 128 * j:128 * j + 128], KST[:, r, 128 * j:128 * j + 128], idn, [bKST[r], bCONST], [bps2],
                           inc=(j == 3))
                    cp(DVE, KT[:, :, nctx + 128 * i:nctx + 128 * i + 128], pv_[:, 0:512].rearrange("p (a b) -> p a b", a=4),
                       [bps2], bKT)

                for i in range(8):
                    ps, bps = psum("A")
                    proj_tm(W, bW, i, ps, bps)
                    if pend_tr[0] is not None:
                        k_transposes(*pend_tr[0])
                    r = i % NKST
                    cp(ACT, KST[:, r, :], ps, [bps], [bKST[r]])
                    if g == 0:
                        s_, t0 = divmod(128 * i, LP)
                        k.dma_out(SP, bKST[r], nk[s_, l, t0:t0 + 128, 512 * blk:512 * blk + 512], KST[:, r, :])
                    pend_tr[0] = (i, r)
                k_transposes(*pend_tr[0])
                if g == 1:
                    for j in range(4):
                        for tc in range(2):
                            src = KT[:, j, nctx + 512 * tc:nctx + 512 * tc + 512]
                            rope_apply(src, [bKT[j]], src, bKT[j], 512 * tc)
                if debug and debug.get("sub") == "k":
                    k.barrier(recycle=False)
                    raise StopBuild()
                W, bW = wnext(("v", l, g, blk))
                if g == 1:
                    k.dma_in(POOL, bV, V[:, 0:4, :], cv[l].rearrange("(t p) c -> p t c", p=128)[:, :, 512 * blk:512 * blk + 512])
                for i in range(8):
                    ps, bps = psum("A")
                    proj_tm(W, bW, i, ps, bps)
                    vt = (nctx // 128) + i
                    if g == 0:
                        r = i % NKST
                        cp(ACT, KST[:, r, :], ps, [bps], [bKST[r]])
                        cp(DVE, V[:, vt, :], KST[:, r, :], [bKST[r]], [bV])
                        s_, t0 = divmod(128 * i, LP)
                        k.dma_out(SP, bKST[r], nv[s_, l, t0:t0 + 128, 512 * blk:512 * blk + 512], KST[:, r, :])
                    else:
                        cp(DVE, V[:, vt, :], ps, [bps], [bV])
                if debug and debug.get("sub") == "v":
                    k.barrier(recycle=False)
                    raise StopBuild()
                if g == 0:
                    iters = [[(s2 * LP, s2 * KL, s2 * nkt_seq) for s2 in (sp, sp + 1)] for sp in (0, 2)]
                    nqs = 256
                else:
                    iters = [[(qc * 512, 0, 0)] for qc in range(2)]
                    nqs = 512
                for j in range(4):
                    h = 4 * blk + j
                    for subs in iters:
                        n_it = itn[0]
                        itn[0] += 1
                        qbase = subs[0][0]
                        Bb_ = AT[:, 1 + n_it % 2, :]
                        bBb = bAT[1 + n_it % 2]
                        Aa_ = AT[:, 0, :]
                        bAa = bAT[0]
                        accs = [psum("B") for _ in range(4)]
                        for c in range(2):
                            Ops, bO = accs[2 * c]
                            Zps, bZ = accs[2 * c + 1]
                            pend = []
                            if g == 1 and not qm_ready[0]:
                                ts(DVE, QM[:, c, :], QT[:, j, qbase:qbase + 512], CST[:, 2 + c:3 + c], None, ALU.mult, None,
                                   [bQT[j], bCST], [bQM[c]])
                            qm_ready[0] = False

                            def emit_s(kt):
                                ps, bps = psum("A")
                                for i_, (q0, k0b, vtb) in enumerate(subs):
                                    k0 = k0b + 128 * kt
                                    if g == 1:
                                        mm(ps[:, 0:512], KT[:, j, k0:k0 + 128], QM[:, c, :], True, True, [bKT[j], bQM[c]], [bps])
                                    else:
                                        mm(ps[:, i_ * nqs:(i_ + 1) * nqs], KT[64 * c:64 * c + 64, j, k0:k0 + 128],
                                           QT[64 * c:64 * c + 64, j, q0:q0 + nqs], True, True, [bKT[j], bQT[j]], [bps],
                                           inc=(i_ == len(subs) - 1))
                                e = en[0] % 4
                                en[0] += 1
                                act(ET[:, e, :], ps, AF.Exp, [bps], [bET[e]], scale=0.125)
                                pend.append((kt, e))

                            if len(subs) > 1:
                                if prestart[0] is not None:
                                    pend.extend(prestart[0])
                                    prestart[0] = None
                                else:
                                    for kt in range(nkt_seq):
                                        emit_s(kt)
                                for i_, (q0, k0b, vtb) in enumerate(subs):
                                    for kt_, e in pend:
                                        mm(Ops[:, i_ * nqs:(i_ + 1) * nqs], V[:, vtb + kt_, 128 * j:128 * j + 128],
                                           ET[:, e, i_ * nqs:(i_ + 1) * nqs], kt_ == 0, kt_ == nkt_seq - 1, [bV, bET[e]], [bO],
                                           inc=False)
                                for kt_, e in pend:
                                    mm(Zps, ONES, ET[:, e, :], kt_ == 0, kt_ == nkt_seq - 1, [bCONST, bET[e]], [bZ], inc=True)
                                if c == 0:
                                    nj, nsubs, ncc = j, subs, 1
                                else:
                                    idx = iters.index(subs)
                                    if idx + 1 < len(iters):
                                        nj, nsubs, ncc = j, iters[idx + 1], 0
                                    elif j + 1 < 4:
                                        nj, nsubs, ncc = j + 1, iters[0], 0
                                    else:
                                        nj = None
                                if nj is not None:
                                    pl = []
                                    for kt0 in range(nkt_seq):
                                        ps, bps = psum("A")
                                        for i_, (q0, k0b, vtb) in enumerate(nsubs):
                                            k0 = k0b + 128 * kt0
                                            mm(ps[:, i_ * nqs:(i_ + 1) * nqs], KT[64 * ncc:64 * ncc + 64, nj, k0:k0 + 128],
                                               QT[64 * ncc:64 * ncc + 64, nj, q0:q0 + nqs], True, True, [bKT[nj], bQT[nj]], [bps],
                                               inc=(i_ == len(nsubs) - 1))
                                        e = en[0] % 4
                                        en[0] += 1
                                        act(ET[:, e, :], ps, AF.Exp, [bps], [bET[e]], scale=0.125)
                                        pl.append((kt0, e))
                                    prestart[0] = pl
                            else:
                                if prestart[0] is not None:
                                    pend.extend(prestart[0])
                                    prestart[0] = None
                                else:
                                    emit_s(0)
                                    if nkt_seq > 1:
                                        emit_s(1)
                                for kt in range(nkt_seq):
                                    if kt + 2 < nkt_seq:
                                        emit_s(kt + 2)
                                    kt_, e = pend.pop(0)
                                    vt = subs[0][2] + kt_
                                    mm(Ops, V[:, vt, 128 * j:128 * j + 128], ET[:, e, :], kt_ == 0, kt_ == nkt_seq - 1,
                                       [bV, bET[e]], [bO], inc=False)
                                    mm(Zps, ONES, ET[:, e, :], kt_ == 0, kt_ == nkt_seq - 1, [bCONST, bET[e]], [bZ], inc=True)
                            if g == 1:
                                if c == 0:
                                    nj, nqb, ncc = j, qbase, 1
                                else:
                                    idx = iters.index(subs)
                                    if idx + 1 < len(iters):
                                        nj, nqb, ncc = j, iters[idx + 1][0][0], 0
                                    elif j + 1 < 4:
                                        nj, nqb, ncc = j + 1, iters[0][0][0], 0
                                    else:
                                        nj = None
                                if nj is not None:
                                    ts(DVE, QM[:, ncc, :], QT[:, nj, nqb:nqb + 512], CST[:, 2 + ncc:3 + ncc], None, ALU.mult, None,
                                       [bQT[nj], bCST], [bQM[ncc]])
                                    qm_ready[0] = True
                                    pl = []
                                    for kt0 in range(2):
                                        ps, bps = psum("A")
                                        mm(ps[:, 0:512], KT[:, nj, 128 * kt0:128 * kt0 + 128], QM[:, ncc, :], True, True,
                                           [bKT[nj], bQM[ncc]], [bps])
                                        e = en[0] % 4
                                        en[0] += 1
                                        act(ET[:, e, :], ps, AF.Exp, [bps], [bET[e]], scale=0.125)
                                        pl.append((kt0, e))
                                    prestart[0] = pl
                            if c == 0 and tail[0] is not None:
                                tail[0]()
                                tail[0] = None
                            act(Aa_, Zps, AF.Ln, [bZ], [bAa])
                            act(Aa_, Aa_, AF.Exp, [bAa], [bAa], scale=-1.0)
                            if c == 0:
                                tt(DVE, Bb_, Ops, Aa_, ALU.mult, [bO, bAa], [bBb])
                            else:
                                tt(DVE, Aa_, Ops, Aa_, ALU.mult, [bO, bAa], [bAa])
                                stt(Bb_, Aa_, LAMS[:, 4 + l:5 + l], Bb_, ALU.mult, ALU.add, [bAa, bBb, bLAMS], [bBb])

                        def make_tail(Bb_=Bb_, bBb=bBb, h=h, qbase=qbase):
                            def _t():
                                tt(DVE, SQA, Bb_, Bb_, ALU.mult, [bBb], [bSQA])
                                ps, bps = psum("A")
                                mm(ps, ONES, SQA, True, True, [bCONST, bSQA], [bps])
                                act(ps, ps, AF.Ln, [bps], [bps], scale=1.0 / 128.0, bias=EPS_AP)
                                act(ps, ps, AF.Exp, [bps], [bps], scale=-0.5)
                                stt(MT[:, h, qbase:qbase + 512], Bb_, SMALL[:, 33:34], ps, ALU.mult, ALU.mult,
                                    [bBb, bSMALL, bps], [bMT[h]])
                            return _t
                        tail[0] = make_tail()
            if tail[0] is not None:
                tail[0]()
                tail[0] = None
            if debug and debug.get("sub") == "heads":
                k.barrier(recycle=False)
                raise StopBuild()
            for blk in range(2):
                W, bW = wnext(("ag", l, g, blk))
                for j in range(4):
                    h = 4 * blk + j
                    for tc in range(2):
                        ps, bps = psum("A")
                        proj_fm(W, bW, j, tc, ps, bps)
                        r = tc
                        act(ET[:, r, :], ps, AF.Silu, [bps], [bET[r]])
                        sl = MT[:, h, 512 * tc:512 * tc + 512]
                        tt(DVE, sl, sl, ET[:, r, :], ALU.mult, [bMT[h], bET[r]], [bMT[h]])
            k.merge_into(bMT[8:14], [bV])
            k.merge_into(bMT[14:16], [bROPE])
            k.soft_barrier()

    def phase_lru(l, g):
        nseq, L = (4, LP) if g == 0 else (1, LS)
        with nc.sbuf_tensor(un("WBD"), [128, 2, 4, 128], F32) as WBDh, \
                nc.sbuf_tensor(un("LW"), [128, 7, TOK], F32) as LWh, nc.sbuf_tensor(un("LX2"), [128, TOK], F32) as LX2h, \
                nc.sbuf_tensor(un("SGb"), [128, TOK], BF16) as SGbh, nc.sbuf_tensor(un("XC2"), [128, TOK], F32) as XC2h:
            WBD, LW, LX2, SGb, XC2 = WBDh[:], LWh[:], LX2h[:], SGbh[:], XC2h[:]
            HL, HLT, bHL, bHLT = HL_P, HLT_P, bHL_P, bHLT_P
            bWBDp = [k.buf("wbd"), k.buf("wbd")]
            bLW = [k.buf("lw") for _ in range(8)]
            bLX2 = k.buf("lx2")
            bXC2 = k.buf("xc2")
            XCs = [LW[:, 0, :], XC2]
            bXCs = [bLW[0], bXC2]
            k.op(DVE, lambda: nc.vector.memset(WBD[:, 0], 0.0), [], [bWBDp[0]])
            k.op(DVE, lambda: nc.vector.memset(WBD[:, 1], 0.0), [], [bWBDp[1]])

            def load_wbd(j):
                first = True
                for gate, wsrc in enumerate((lru_wa, lru_wi)):
                    for d in range(2):
                        for half in range(2):
                            dst = WBD[64 * half:64 * half + 64, j % 2, gate * 2 + d, 64 * half:64 * half + 64]
                            src = wsrc[l, d, 2 * j + half]
                            if first:
                                k.dma_in(SP, bWBDp[j % 2], dst, src)
                                first = False
                            else:
                                k.dma_in_more(SP, bWBDp[j % 2], dst, src)

            load_wbd(0)
            load_wbd(1)
            act(SMALL[:, 16:24], PVT[l][:, R_LLAM:R_LLAM + 8], AF.Exp, [bPVT[l]], [bSMALL], scale=-1.0)
            act(SMALL[:, 16:24], SMALL[:, 16:24], AF.Ln, [bSMALL], [bSMALL], bias=ONE_AP)
            ts(DVE, SMALL[:, 24:32], SMALL[:, 16:24], -16.0, None, ALU.mult, None, [bSMALL], [bSMALL])
            ts(DVE, SMALL[:, 16:24], SMALL[:, 16:24], -8.0, None, ALU.mult, None, [bSMALL], [bSMALL])
            Wx, bWx = wnext(("lx", l, g))
            Wg, bWg = wnext(("lg", l, g), prefetch=False)
            pv = PVT[l]
            def lx_proj(j):
                for tc in range(2):
                    ps, bps = psum("A")
                    proj_fm(Wx, bWx, j, tc, ps, bps)
                    cp(ACT, LX2[:, 512 * tc:512 * tc + 512], ps, [bps], [bLX2])

            def conv(j):
                XCj, bXCj = XCs[j % 2], bXCs[j % 2]
                x3 = LX2.rearrange("p (s t) -> p s t", s=nseq)
                xc3 = XCj.rearrange("p (s t) -> p s t", s=nseq)

                def cw(kk):
                    c_ = R_LCW + 4 * kk + j
                    return pv[:, c_:c_ + 1]
                ts(DVE, XCj, LX2, cw(2), pv[:, R_LCB + j:R_LCB + j + 1], ALU.mult, ALU.add, [bLX2, bPVT[l]], [bXCj])
                stt(xc3[:, :, 2:L], x3[:, :, 0:L - 2], cw(0), xc3[:, :, 2:L], ALU.mult, ALU.add, [bLX2, bPVT[l], bXCj], [bXCj])
                stt(xc3[:, :, 1:L], x3[:, :, 0:L - 1], cw(1), xc3[:, :, 1:L], ALU.mult, ALU.add, [bLX2, bPVT[l], bXCj], [bXCj])
                stt(xc3[:, :, 0:L - 1], x3[:, :, 1:L], cw(3), xc3[:, :, 0:L - 1], ALU.mult, ALU.add, [bLX2, bPVT[l], bXCj], [bXCj])

            lx_proj(0)
            conv(0)
            for j in range(4):
                _, Rr, Ii, Aa, Bb, Hh, HS = [LW[:, i_, :] for i_ in range(7)]
                XC = XCs[j % 2]
                bLW[0] = bXCs[j % 2]
                bWBD = bWBDp[j % 2]
                SG = SGb
                for tc in range(2):
                    ps, bps = psum("A")
                    proj_fm(Wg, bWg, j, tc, ps, bps)
                    act(SG[:, 512 * tc:512 * tc + 512], ps, AF.Silu, [bps], [bLW[7]])
                if j + 1 < 4:
                    lx_proj(j + 1)
                    conv(j + 1)
                for d in range(2):
                    for tc in range(2):
                        ps, bps = psum("A")
                        mm(ps, WBD[:, j % 2, 0 * 2 + d, :], XC[:, 512 * tc:512 * tc + 512], True, True, [bWBD, bLW[0]], [bps])
                        act(Rr[:, 512 * tc:512 * tc + 512], ps, AF.Sigmoid, [bps, bPVT[l]], [bLW[1]],
                            bias=pv[:, R_LBA + 4 * d + j:R_LBA + 4 * d + j + 1])
                        ps, bps = psum("A")
                        mm(ps, WBD[:, j % 2, 1 * 2 + d, :], XC[:, 512 * tc:512 * tc + 512], True, True, [bWBD, bLW[0]], [bps])
                        act(Ii[:, 512 * tc:512 * tc + 512], ps, AF.Sigmoid, [bps, bPVT[l]], [bLW[2]],
                            bias=pv[:, R_LBI + 4 * d + j:R_LBI + 4 * d + j + 1])
                    cn = SMALL[:, 16 + 4 * d + j:17 + 4 * d + j]
                    cn2 = SMALL[:, 24 + 4 * d + j:25 + 4 * d + j]
                    act(Aa, Rr, AF.Exp, [bLW[1], bSMALL], [bLW[3]], scale=cn)
                    act(Bb, Rr, AF.Exp, [bLW[1], bSMALL], [bLW[4]], scale=cn2)
                    act(Bb, Bb, AF.Sqrt, [bLW[4]], [bLW[4]], scale=-1.0, bias=ONE_AP)
                    tt(DVE, Ii, Ii, XC, ALU.mult, [bLW[2], bLW[0]], [bLW[2]])
                    tt(DVE, Bb, Bb, Ii, ALU.mult, [bLW[4], bLW[2]], [bLW[4]])
                    for s_ in range(nseq):
                        sl = slice(s_ * L, s_ * L + L)
                        if g == 1:
                            init = pv[:, R_ST + 4 * d + j:R_ST + 4 * d + j + 1]
                        else:
                            init = 0.0
                        rd = [bLW[3], bLW[4]] + ([bPVT[l]] if g == 1 else [])
                        if d == 0:
                            k.op(DVE, lambda: nc.vector.tensor_tensor_scan(Hh[:, sl], Aa[:, sl], Bb[:, sl], init, ALU.mult, ALU.add),
                                 rd, [bLW[5]])
                        else:
                            k.op(DVE, lambda: nc.vector.tensor_tensor_scan(Hh[:, sl][:, ::-1], Aa[:, sl][:, ::-1], Bb[:, sl][:, ::-1],
                                                                          init, ALU.mult, ALU.add), rd, [bLW[5]])
                        if g == 0:
                            col = (s_ * 2 + d) * 4 + j
                            tcol = s_ * L + (L - 1 if d == 0 else 0)
                            cp(DVE, HL[:, col:col + 1], Hh[:, tcol:tcol + 1], [bLW[5]], [bHL])
                    if d == 0:
                        cp(DVE, HS, Hh, [bLW[5]], [bLW[6]])
                    else:
                        tt(DVE, HS, HS, Hh, ALU.add, [bLW[6], bLW[5]], [bLW[6]])
                tt(DVE, MT[:, 8 + j, :], HS, SG, ALU.mult, [bLW[6], bLW[7]], [bMT[8 + j]])
                if j + 2 < 4:
                    load_wbd(j + 2)
            wprefetch()
            if g == 0:
                def _hl_out():
                    ps, bps = psum("A")
                    tr(ps[0:32, 0:128], HL, IDF, [bHL, bCONST], [bps])
                    cp(DVE, HLT, ps[0:32, 0:128], [bps], [bHLT])
                    for s_ in range(4):
                        k.dma_out(SP, bHLT, nst[s_, l].rearrange("d (j p) -> (d j) p", p=128), HLT[8 * s_:8 * s_ + 8, :])
                deferred.append(_hl_out)
            k.soft_barrier()

    def wrap_sin(dst, bdst, ps, bps, bias_ap, x1, t1, bx1, bt1, extra_reads):
        ts(DVE, x1, ps, bias_ap, None, ALU.add, None, [bps] + extra_reads, [bx1])
        ts(DVE, t1, x1, PI, -2.0 * PI, ALU.is_gt, ALU.mult, [bx1], [bt1])
        tt(DVE, x1, x1, t1, ALU.add, [bx1, bt1], [bx1])
        ts(DVE, t1, x1, -PI, 2.0 * PI, ALU.is_lt, ALU.mult, [bx1], [bt1])
        tt(DVE, x1, x1, t1, ALU.add, [bx1, bt1], [bx1])
        ts(DVE, x1, x1, PI, -PI, ALU.min, ALU.max, [bx1], [bx1])
        act(dst, x1, AF.Sin, [bx1], [bdst])

    def phase_hyena(l, g):
        nseq, L = (4, LP) if g == 0 else (1, LS)
        gi = g
        T = L // 128
        npiece = (2 * L) // 512
        NKC = (2 * L) // 128
        pv = PVT[l]

        def conv3(dst, bdst, src, bsrc, m):
            def cw(kk):
                c_ = R_HCW + 12 * kk + m
                return pv[:, c_:c_ + 1]
            s3 = src.rearrange("p (s t) -> p s t", s=nseq)
            d3 = dst.rearrange("p (s t) -> p s t", s=nseq)
            act(dst, src, AF.Identity, [bsrc, bPVT[l]], [bdst], scale=cw(1), bias=pv[:, R_HCB + m:R_HCB + m + 1])
            stt(d3[:, :, 1:L], s3[:, :, 0:L - 1], cw(0), d3[:, :, 1:L], ALU.mult, ALU.add, [bsrc, bPVT[l], bdst], [bdst])
            stt(d3[:, :, 0:L - 1], s3[:, :, 1:L], cw(2), d3[:, :, 0:L - 1], ALU.mult, ALU.add, [bsrc, bPVT[l], bdst], [bdst])

        with nc.sbuf_tensor(un("G"), [128, NKC, 512], BF16) as Gh:
            G = Gh[:]
            bG = [k.buf("g") for _ in range(NKC)]
            bGall = k.buf("gall")
            k.dma_in(SP, bGall, G, gscr[g][l])
            for kc_ in range(NKC):
                bG[kc_].w = dict(bGall.w)
            with nc.sbuf_tensor(un("UT"), [128, TOK // 128, 512], BF16) as UTh:
                UT = UTh[:]
                bUT = k.buf("ut")
                with nc.sbuf_tensor(un("HW"), [128, 3, TOK], F32) as HWh, nc.sbuf_tensor(un("CV"), [128, 4, TOK], BF16) as CVh:
                    HW, CV = HWh[:], CVh[:]
                    bHW = [k.buf("hw") for _ in range(3)]
                    bCV = [k.buf("cv") for _ in range(4)]
                    W, bW = wnext(("hv", l, g))
                    for j in range(4):
                        rs = j % 2
                        for tc in range(2):
                            ps, bps = psum("A")
                            proj_fm(W, bW, j, tc, ps, bps)
                            cp(ACT, HW[:, rs, 512 * tc:512 * tc + 512], ps, [bps], [bHW[rs]])
                        conv3(HW[:, 2, :], bHW[2], HW[:, rs, :], bHW[rs], j)
                        cp(DVE, CV[:, j, :], HW[:, 2, :], [bHW[2]], [bCV[j]])
                    for fn_ in deferred:
                        fn_()
                    deferred.clear()
                    W, bW = wnext(("hx1", l, g))

                    def hx1_proj(j):
                        rs = j % 2
                        for tc in range(2):
                            ps, bps = psum("A")
                            proj_fm(W, bW, j, tc, ps, bps)
                            cp(ACT, HW[:, rs, 512 * tc:512 * tc + 512], ps, [bps], [bHW[rs]])

                    hx1_proj(0)
                    for j in range(4):
                        rs = j % 2
                        conv3(HW[:, 2, :], bHW[2], HW[:, rs, :], bHW[rs], 4 + j)
                        if j + 1 < 4:
                            hx1_proj(j + 1)
                        tt(DVE, CV[:, j, :], CV[:, j, :], HW[:, 2, :], ALU.mult, [bCV[j], bHW[2]], [bCV[j]])
                        for q2 in range(2):
                            ps, bps = psum("A")
                            psb = ps.bitcast(BF16)
                            for i4 in range(4):
                                i = 4 * q2 + i4
                                tr(psb[:, 128 * i4:128 * i4 + 128], CV[:, j, 128 * i:128 * i + 128], IDB, [bCV[j], bCONST], [bps],
                                   inc=(i4 == 3))
                            cp(ACT, UT[:, 4 * q2:4 * q2 + 4, 128 * j:128 * j + 128],
                               psb[:, 0:512].rearrange("p (a b) -> p a b", a=4), [bps], [bUT])
                    k.soft_barrier()
                npp = nseq * NKC if g == 0 else 1
                with nc.sbuf_tensor(un("PP"), [128, npp, 512], BF16) as PPh, nc.sbuf_tensor(un("TP"), [128, 4, 512], F32) as TPh:
                    PP, TP = PPh[:], TPh[:]
                    bPP = [k.buf("pp") for _ in range(npp)]
                    bTP = [k.buf("tp") for _ in range(4)]

                    def Pslot(s_, kc):
                        if g == 0:
                            return PP[:, s_ * NKC + kc, :], bPP[s_ * NKC + kc]
                        return G[:, kc, :], bG[kc]

                    for p in range(npiece):
                        W, bW = wnext(("Ff", l, g, p))
                        for s_ in range(nseq):
                            for q in range(2):
                                kcc, kcs = 4 * p + q, 4 * p + 2 + q
                                psc, bpsc = psum("A")
                                pss_, bpss = psum("A")
                                for t in range(T):
                                    mm(psc, W[:, t, 128 * q:128 * q + 128], UT[:, s_ * T + t, :], t == 0, t == T - 1, [bW, bUT], [bpsc])
                                for t in range(T):
                                    mm(pss_, W[:, t, 256 + 128 * q:256 + 128 * q + 128], UT[:, s_ * T + t, :], t == 0, t == T - 1,
                                       [bW, bUT], [bpss])
                                Gc, Gs = G[:, kcc, :], G[:, kcs, :]
                                tt(DVE, TP[:, 0, :], psc, Gc, ALU.mult, [bpsc, bG[kcc]], [bTP[0]])
                                tt(DVE, TP[:, 1, :], pss_, Gs, ALU.mult, [bpss, bG[kcs]], [bTP[1]])
                                tt(DVE, TP[:, 2, :], pss_, Gc, ALU.mult, [bpss, bG[kcc]], [bTP[2]])
                                tt(DVE, TP[:, 3, :], psc, Gs, ALU.mult, [bpsc, bG[kcs]], [bTP[3]])
                                Pc, bPc = Pslot(s_, kcc)
                                Ps_, bPs = Pslot(s_, kcs)
                                tt(DVE, Pc, TP[:, 0, :], TP[:, 1, :], ALU.subtract, [bTP[0], bTP[1]], [bPc])
                                tt(DVE, Ps_, TP[:, 2, :], TP[:, 3, :], ALU.add, [bTP[2], bTP[3]], [bPs])
                    if g == 0:
                        W, bW = wnext(("Fi", l, g, 0))
                        for s_ in range(nseq):
                            for j in range(4):
                                ps, bps = psum("A")
                                for kc in range(NKC):
                                    Pk, bPk = Pslot(s_, kc)
                                    mm(ps[:, 0:L], Pk[:, 128 * j:128 * j + 128], W[:, kc, 0:L], kc == 0, kc == NKC - 1, [bPk, bW], [bps])
                                cp(ACT, MT[:, 12 + j, s_ * L:s_ * L + L], ps[:, 0:L], [bps], [bMT[12 + j]])
                    else:
                        for hh in range(2):
                            W, bW = wnext(("Fi", l, g, hh))
                            for j in range(4):
                                ps, bps = psum("A")
                                for kc in range(NKC):
                                    Pk, bPk = Pslot(0, kc)
                                    mm(ps, Pk[:, 128 * j:128 * j + 128], W[:, kc, :], kc == 0, kc == NKC - 1, [bPk, bW], [bps])
                                cp(ACT, MT[:, 12 + j, 512 * hh:512 * hh + 512], ps, [bps], [bMT[12 + j]])
                    k.soft_barrier()
        with nc.sbuf_tensor(un("HW4"), [128, 4, TOK], F32) as HWh, nc.sbuf_tensor(un("SG4"), [128, 2, 512], BF16) as SGh:
            HW, SG = HWh[:], SGh[:]
            bHW = [k.buf("hw") for _ in range(4)]
            bSG = [k.buf("sg") for _ in range(2)]
            W, bW = wnext(("hx0", l, g))
            for j in range(4):
                for tc in range(2):
                    ps, bps = psum("A")
                    proj_fm(W, bW, j, tc, ps, bps)
                    cp(ACT, HW[:, j % 2, 512 * tc:512 * tc + 512], ps, [bps], [bHW[j % 2]])
                conv3(HW[:, 2 + j % 2, :], bHW[2 + j % 2], HW[:, j % 2, :], bHW[j % 2], 8 + j)
                tt(DVE, MT[:, 12 + j, :], MT[:, 12 + j, :], HW[:, 2 + j % 2, :], ALU.mult, [bMT[12 + j], bHW[2 + j % 2]],
                   [bMT[12 + j]])
            W, bW = wnext(("hg", l, g))
            for j in range(4):
                for tc in range(2):
                    ps, bps = psum("A")
                    proj_fm(W, bW, j, tc, ps, bps)
                    r = tc
                    act(SG[:, r, :], ps, AF.Silu, [bps], [bSG[r]])
                    sl = MT[:, 12 + j, 512 * tc:512 * tc + 512]
                    tt(DVE, sl, sl, SG[:, r, :], ALU.mult, [bMT[12 + j], bSG[r]], [bMT[12 + j]])
            k.soft_barrier()

    def phase_out(l, g, bg=None, stats=None):
        rn = "O" if stats is None else "O3"
        sn = 0
        for t in range(4):
            W, bW = wnext(("wo", l, g, t))
            for j in range(4):
                dc = 4 * t + j
                for tc in range(2):
                    if bg is not None:
                        next(bg, None)
                    ps, bps = psum(rn)
                    for kc in range(16):
                        mm(ps, W[:, kc, 128 * j:128 * j + 128], MT[:, kc, 512 * tc:512 * tc + 512], kc == 0, kc == 15,
                           [bW, bMT[kc]], [bps])
                    sl = XT[:, dc, 512 * tc:512 * tc + 512]
                    stt(sl, ps, MODT[l][:, 32 + dc, g:g + 1], sl, ALU.mult, ALU.add, [bps, bMODT[l], bXT[dc]], [bXT[dc]])
                    if stats is not None:
                        SQ_, bSQ_, pss_ = stats["SQ"], stats["bSQ"], stats["pss"]
                        r_ = sn % 4
                        sn += 1
                        act(SQ_[:, r_, :], sl, AF.Square, [bXT[dc]], [bSQ_[r_]])
                        mm(pss_[tc][0], ONES, SQ_[:, r_, :], dc == 0, dc == 15, [bSQ_[r_], bCONST], [pss_[tc][1]], inc=True)
        if stats is not None:
            RB_, bRB_ = stats["RB"], stats["bRB"]
            for tc in range(2):
                act(RB_[:, 512 * tc:512 * tc + 512], stats["pss"][tc][0], AF.Ln, [stats["pss"][tc][1]], [bRB_],
                    scale=1.0 / D, bias=EPS_AP)
            act(RB_, RB_, AF.Exp, [bRB_], [bRB_], scale=-0.5)
        if bg is not None:
            for _ in bg:
                pass
        k.soft_barrier()

    def phase_final(g, RB, bRB):
        with nc.sbuf_tensor(un("YST"), [128, 2, D], F32) as YSTh:
            YST = YSTh[:]
            bYST = [k.buf("yst"), k.buf("yst")]
            for dc in range(16):
                stt(XT[:, dc, :], XT[:, dc, :], GVT[:, 64 + dc:65 + dc], RB, ALU.mult, ALU.mult, [bXT[dc], bGVT, bRB], [bXT[dc]])
            for i in range(8):
                r = i % 2
                for q4 in range(4):
                    ps, bps = psum("ALL")
                    for j in range(4):
                        dc = 4 * q4 + j
                        tr(ps[:, 128 * j:128 * j + 128], XT[:, dc, 128 * i:128 * i + 128], IDF, [bXT[dc], bCONST], [bps],
                           inc=(j == 3))
                    E = ACT if q4 % 2 == 0 else DVE
                    cp(E, YST[:, r, 512 * q4:512 * q4 + 512], ps, [bps], [bYST[r]])
                k.dma_out(SP, bYST[r], yout[g][128 * i:128 * i + 128, :], YST[:, r, :])
            k.barrier()

    HL_P = sb("HL_P", [128, 32], F32)
    HLT_P = sb("HLT_P", [32, 128], F32)
    bHL_P = k.buf("hl")
    bHLT_P = k.buf("hlt")
    k.dma_keep = None
    deferred = []
    CST = sb("CST", [128, 4], F32)
    bCST = k.buf("cst")
    k.op(DVE, lambda: nc.vector.memset(CST[:, 0:1], EPS), [], [bCST])
    k.op(DVE, lambda: nc.vector.memset(CST[:, 1:2], 1.0), [], [bCST])
    k.op(DVE, lambda: nc.vector.memset(CST[:, 2:4], 0.0), [], [bCST])
    k.op(DVE, lambda: nc.vector.memset(CST[0:64, 2:3], 1.0), [], [bCST])
    k.op(DVE, lambda: nc.vector.memset(CST[64:128, 3:4], 1.0), [], [bCST])
    EPS_AP = CST[:, 0:1]
    ONE_AP = CST[:, 1:2]
    k.barrier()

    phase_ada()
    dump("modt0", MODT[0].rearrange("p a b -> p (a b)"), bMODT[0], [128, 96])
    stop = debug.get("stop") if debug else None

    def checkpoint(name, what):
        if stop == name:
            if what == "mt":
                dump("mt", MT.rearrange("p a b -> p (a b)"), bMT, [128, 16 * TOK], BF16)
            elif what == "xt":
                dump("xt", XT.rearrange("p a b -> p (a b)"), bXT, [128, 16 * TOK], F32)
            elif what == "ht":
                dump("ht", HT.rearrange("p a b -> p (a b)"), bHT, [128, 16 * TOK], BF16)
            raise StopBuild()

    def _run_group(g):
        if g != 0:
            load_x(g)
        for l in range(DEPTH):
            if (g, l) != (0, 0):
                phase_norm(l, g)
            checkpoint(f"norm{g}{l}", "ht")
            phase_attn(l, g)
            checkpoint(f"attn{g}{l}", "mt")
            phase_lru(l, g)
            checkpoint(f"lru{g}{l}", "mt")
            phase_hyena(l, g)
            checkpoint(f"hy{g}{l}", "mt")
            nxt = {(0, 0): (0, 1), (0, 1): (1, 0), (1, 0): (1, 1)}.get((g, l))
            if l < DEPTH - 1:
                phase_out(l, g, gen_G(*nxt) if nxt else None)
            else:
                with nc.sbuf_tensor(un("RBF"), [128, TOK], F32) as RBh, nc.sbuf_tensor(un("SQF"), [128, 4, 512], BF16) as SQh:
                    st = {"RB": RBh[:], "bRB": k.buf("rbf"), "SQ": SQh[:], "bSQ": [k.buf("sqf") for _ in range(4)],
                          "pss": [(PS[3], bPS[3]), (PS[7], bPS[7])]}
                    phase_out(l, g, gen_G(*nxt) if nxt else None, stats=st)
                    phase_final(g, st["RB"], st["bRB"])
            checkpoint(f"out{g}{l}", "xt")

    try:
        for g in range(2):
            _run_group(g)
    except StopBuild:
        if debug and debug.get("sub"):
            dump("mt", MT.rearrange("p a b -> p (a b)"), bMT, [128, 16 * TOK], BF16)
    k.barrier([SP], recycle=False)
    return nc, dbg_outs


_CONSTS = None


def _consts():
    global _CONSTS
    if _CONSTS is None:
        F0, Fi0 = _dft_tables(LP)
        F1, Fi1 = _dft_tables(LS)
        z0, d0 = _hy_consts(LP)
        z1, d1 = _hy_consts(LS)
        cosT, sinT, RT = _rope_tables()
        _CONSTS = {
            "c_idf": np.eye(128, dtype=np.float32),
            "c_idb": np.eye(128).astype(ml_dtypes.bfloat16),
            "c_rt": RT, "c_cos": cosT, "c_sin": sinT,
            "c_z0": z0, "c_z1": z1, "c_dec0": d0, "c_dec1": d1,
            "c_F0": F0, "c_F1": F1, "c_Fi0": Fi0, "c_Fi1": Fi1,
        }
    return _CONSTS


def make_in_maps(inputs):
    f = lambda a: np.ascontiguousarray(np.asarray(a, dtype=np.float32))
    shared = {n: f(inputs[n]) for n in (
        "norm_g", "w_ada", "b_ada", "w_in", "w_out", "lam_q1", "lam_k1", "lam_q2", "lam_k2", "attn_subln_g",
        "lru_conv_w", "lru_conv_b", "lru_wa", "lru_ba", "lru_wi", "lru_bi", "lru_lam", "hy_conv_w", "hy_conv_b",
        "hy_w1", "hy_b1", "hy_w2", "hy_b2", "hy_w3", "hy_d", "final_g")}
    shared.update(_consts())
    xp = f(inputs["x_prompt"])
    xs = f(inputs["x_sample"])
    ckk = f(inputs["cache_k"])
    cvv = f(inputs["cache_v"])
    stl = f(inputs["state_lru"])
    c = f(inputs["c"])
    cctx = f(inputs["c_ctx"])
    maps = []
    for j in range(NCORES):
        s = j % 2
        m = dict(shared)
        m["xp"] = np.ascontiguousarray(xp[4 * j:4 * j + 4].reshape(TOK, D))
        m["xs"] = np.ascontiguousarray(xs[s])
        m["ck"] = np.ascontiguousarray(ckk[s].reshape(DEPTH, PAST, 1024))
        m["cv"] = np.ascontiguousarray(cvv[s].reshape(DEPTH, PAST, 1024))
        m["st_lru"] = np.ascontiguousarray(stl[s])
        m["cvec"] = np.ascontiguousarray(np.stack([cctx, c[s]], axis=0))
        maps.append(m)
    return maps


def kernel(**inputs):
    nc, _ = build_program()
    maps = make_in_maps(inputs)
    res = run_bass_kernel_spmd(nc, maps, core_ids=list(range(NCORES)))
    R = res.results
    y_prompt = np.concatenate([R[j]["yp"].reshape(4, LP, D) for j in range(NCORES)], axis=0).astype(np.float32)
    y_sample = np.stack([R[0]["ys"], R[1]["ys"]], axis=0).astype(np.float32)
    nk_ = np.concatenate([R[j]["nk"].reshape(4, DEPTH, LP, NH, 128) for j in range(NCORES)], axis=0).astype(np.float32)
    nv_ = np.concatenate([R[j]["nv"].reshape(4, DEPTH, LP, NH, 128) for j in range(NCORES)], axis=0).astype(np.float32)
    nst_ = np.concatenate([R[j]["nst"] for j in range(NCORES)], axis=0).astype(np.float32)
    return (y_prompt, y_sample, nk_, nv_, nst_)
```

```python
import math
import numpy as np
import ml_dtypes
import concourse.bass as bass
import concourse.mybir as mybir
from concourse.bass_utils import run_bass_kernel_spmd

F32 = mybir.dt.float32
BF16 = mybir.dt.bfloat16
AF = mybir.ActivationFunctionType
ALU = mybir.AluOpType

D = 2048
NCORES = 8
DEPTH = 2
LP = 256
LS = 1024
PAST = 512
NH = 8
INW = 7168
EPS = 1e-6
TOK = 1024
PI = math.pi

R_LCW, R_LCB, R_LBA, R_LBI, R_LLAM, R_HCW, R_HCB, R_ST, R_SUB, R_B1, R_B2 = 0, 16, 20, 28, 36, 44, 80, 92, 100, 101, 102


def _dft_tables(L):
    n = 2 * L
    t = np.arange(L, dtype=np.float64)
    k = np.arange(L, dtype=np.float64)
    ang = 2.0 * np.pi * (k[None, :] + 0.5) * t[:, None] / n
    C = np.cos(ang)
    S = np.sin(ang)
    npiece = (2 * L) // 512
    cols = []
    for p in range(npiece):
        ks = slice(256 * p, 256 * p + 256)
        cols.append(C[:, ks])
        cols.append(S[:, ks])
    Fm = np.concatenate(cols, axis=1)
    Finv = (2.0 / n) * Fm.T
    return Fm.astype(ml_dtypes.bfloat16), Finv.astype(ml_dtypes.bfloat16)


def _hy_consts(L):
    pos = np.arange(L, dtype=np.float32)
    t = pos / np.float32(max(L - 1, 1))
    bands = np.linspace(1e-4, 16 - 1, 16, dtype=np.float32)
    ang = (np.float32(2.0 * math.pi / L) * pos[:, None] * bands[None, :]).astype(np.float32)
    z = np.concatenate([t[:, None], np.cos(ang), np.sin(ang)], axis=-1).astype(np.float32)
    lo = abs(math.log(1e-2) / 1.5)
    hi = abs(math.log(1e-2) / 0.3)
    deltas = np.linspace(lo, hi, 512, dtype=np.float32)
    decay = np.exp(-t[:, None] * deltas[None, :]).astype(np.float32)
    return np.ascontiguousarray(z.T), decay


def _rope_tables():
    nf = 16
    inv = (10000.0 ** (-np.arange(nf, dtype=np.float32) / nf)).astype(np.float32)
    tok = np.arange(LS)
    r = (tok // 64).astype(np.float32)
    c = (tok % 64).astype(np.float32)
    cosT = np.zeros((128, LS), np.float32)
    sinT = np.zeros((128, LS), np.float32)
    for p in range(128):
        half = (p % 64) // 32
        i = p % 16
        posv = r if half == 0 else c
        a = (posv * inv[i]).astype(np.float32)
        cosT[p] = np.cos(a)
        sinT[p] = np.sin(a)
    RT = np.zeros((128, 128), np.float32)
    for blk in range(4):
        for i in range(16):
            a_i = blk * 32 + i
            b_i = blk * 32 + 16 + i
            RT[b_i, a_i] = -1.0
            RT[a_i, b_i] = 1.0
    return cosT.astype(ml_dtypes.bfloat16), sinT.astype(ml_dtypes.bfloat16), RT.astype(ml_dtypes.bfloat16)


class Sem:
    def __init__(self, nc, name):
        self.h = nc.alloc_semaphore(name)
        self.count = 0


class Eng:
    def __init__(self, nc, name, e):
        self.name = name
        self.e = e
        self.sem = Sem(nc, "es_" + name)
        self.waited = {}


class Buf:
    __slots__ = ("name", "w", "r", "wsem", "rsem", "wsem_sw")

    def __init__(self, name):
        self.name = name
        self.w = {}
        self.r = {}
        self.wsem = None
        self.rsem = None
        self.wsem_sw = None


STRICT = True
SOFT = True


class StopBuild(Exception):
    pass


class K:
    def __init__(self, nc):
        self.nc = nc
        self.PE = Eng(nc, "pe", nc.tensor)
        self.ACT = Eng(nc, "act", nc.scalar)
        self.DVE = Eng(nc, "dve", nc.vector)
        self.POOL = Eng(nc, "pool", nc.gpsimd)
        self.SP = Eng(nc, "sp", nc.sync)
        self.engs = [self.PE, self.ACT, self.DVE, self.POOL, self.SP]
        self.sems = [e.sem for e in self.engs]
        self.nbuf = 0
        self.free_sems = []
        self.phase_sems = []
        self.carry = {}
        self.scope_bufs = []

    def get_sem(self, name):
        if self.free_sems:
            sm = self.free_sems.pop()
        else:
            sm = Sem(self.nc, name)
            self.sems.append(sm)
        self.phase_sems.append(sm)
        return sm

    def end_phase(self):
        self.free_sems.extend(self.phase_sems)
        self.phase_sems = []

    def keep(self, buf):
        for sm in (buf.wsem, buf.rsem):
            if sm is not None and sm in self.phase_sems:
                self.phase_sems.remove(sm)

    def buf(self, name="b"):
        self.nbuf += 1
        b = Buf(f"{name}{self.nbuf}")
        b.r = dict(self.carry)
        self.scope_bufs.append(b)
        return b

    def soft_barrier(self):
        if not SOFT:
            return self.barrier()
        for b in self.scope_bufs:
            self._acc(self.carry, b.w)
            self._acc(self.carry, b.r)
        self.scope_bufs = []

    def merge_into(self, dst_bufs, src_bufs):
        for d_ in dst_bufs:
            for s_ in src_bufs:
                self._acc(d_.r, s_.w)
                self._acc(d_.r, s_.r)

    def _wait(self, E, deps):
        for sem, val in deps.items():
            if E.waited.get(sem, 0) < val:
                E.e.wait_ge(sem.h, val)
                E.waited[sem] = val

    @staticmethod
    def _acc(deps, d, skip=None):
        for sem, val in d.items():
            if sem is skip:
                continue
            if deps.get(sem, 0) < val:
                deps[sem] = val

    def op(self, E, fn, reads=(), writes=(), inc=True):
        deps = {}
        for b in reads:
            self._acc(deps, b.w)
        for b in writes:
            self._acc(deps, b.w, skip=(None if STRICT else E.sem))
            self._acc(deps, b.r, skip=(None if STRICT else E.sem))
        if deps.get(E.sem, 0) > E.sem.count:
            deps[E.sem] = E.sem.count
        self._wait(E, deps)
        ins = fn()
        if inc:
            E.sem.count += 1
            ins.then_inc(E.sem.h, 1)
            val = E.sem.count
        else:
            val = E.sem.count + 1
        for b in reads:
            if b.r.get(E.sem, 0) < val:
                b.r[E.sem] = val
        for b in writes:
            b.w = {E.sem: val}
            b.r = {}
        return ins

    def dma_in(self, Q, buf, out_ap, in_ap, persist=False):
        deps = {}
        self._acc(deps, buf.w)
        self._acc(deps, buf.r)
        self._wait(Q, deps)
        if Q is self.POOL:
            if buf.wsem_sw is None:
                buf.wsem_sw = Sem(self.nc, "dsw_" + buf.name)
                self.sems.append(buf.wsem_sw)
            sm = buf.wsem_sw
        else:
            if buf.wsem is None:
                buf.wsem = self.get_sem("dw_" + buf.name)
                if persist:
                    self.phase_sems.remove(buf.wsem)
            sm = buf.wsem
        sm.count += 16
        Q.e.dma_start(out=out_ap, in_=in_ap).then_inc(sm.h, 16)
        buf.w = {sm: sm.count}
        buf.r = {}

    def dma_in_more(self, Q, buf, out_ap, in_ap):
        buf.wsem.count += 16
        Q.e.dma_start(out=out_ap, in_=in_ap).then_inc(buf.wsem.h, 16)
        buf.w = {buf.wsem: buf.wsem.count}

    def dma_out(self, Q, buf, out_ap, in_ap):
        deps = {}
        self._acc(deps, buf.w)
        self._wait(Q, deps)
        if buf.rsem is None:
            buf.rsem = self.get_sem("dr_" + buf.name)
        buf.rsem.count += 16
        Q.e.dma_start(out=out_ap, in_=in_ap).then_inc(buf.rsem.h, 16)
        buf.r[buf.rsem] = buf.rsem.count

    def barrier(self, engs=None, recycle=True):
        if engs is None and recycle:
            self._barrier(None)
            self.end_phase()
            self.carry = {}
            self.scope_bufs = []
        else:
            self._barrier(engs)

    def _barrier(self, engs=None):
        for E in (engs or self.engs):
            for s in self.sems:
                if s is E.sem:
                    continue
                if E.waited.get(s, 0) < s.count:
                    E.e.wait_ge(s.h, s.count)
                    E.waited[s] = s.count


def build_program(debug=None):
    nc = bass.Bass("TRN2", target_bir_lowering=False)
    k = K(nc)
    PE, ACT, DVE, POOL, SP = k.PE, k.ACT, k.DVE, k.POOL, k.SP

    def din(name, shape, dt=F32):
        return nc.dram_tensor(name, list(shape), dt, kind="ExternalInput").ap()

    def dout(name, shape, dt=F32):
        return nc.dram_tensor(name, list(shape), dt, kind="ExternalOutput").ap()

    xin = [din("xp", [TOK, D]), din("xs", [TOK, D])]
    ck = din("ck", [DEPTH, PAST, 1024])
    cv = din("cv", [DEPTH, PAST, 1024])
    st_lru = din("st_lru", [DEPTH, 2, 512])
    cvec = din("cvec", [2, D])
    norm_g = din("norm_g", [DEPTH, D])
    w_ada = din("w_ada", [DEPTH, D, 3 * D])
    b_ada = din("b_ada", [DEPTH, 3 * D])
    w_in = din("w_in", [DEPTH, D, INW])
    w_out = din("w_out", [DEPTH, D, D])
    lamv = [din(n, [DEPTH, 64]) for n in ("lam_q1", "lam_k1", "lam_q2", "lam_k2")]
    subln = din("attn_subln_g", [DEPTH, 128])
    lru_conv_w = din("lru_conv_w", [DEPTH, 4, 512])
    lru_conv_b = din("lru_conv_b", [DEPTH, 512])
    lru_wa = din("lru_wa", [DEPTH, 2, 8, 64, 64])
    lru_ba = din("lru_ba", [DEPTH, 2, 512])
    lru_wi = din("lru_wi", [DEPTH, 2, 8, 64, 64])
    lru_bi = din("lru_bi", [DEPTH, 2, 512])
    lru_lam = din("lru_lam", [DEPTH, 2, 512])
    hy_conv_w = din("hy_conv_w", [DEPTH, 3, 1536])
    hy_conv_b = din("hy_conv_b", [DEPTH, 1536])
    hy_w1 = din("hy_w1", [DEPTH, 33, 64])
    hy_b1 = din("hy_b1", [DEPTH, 64])
    hy_w2 = din("hy_w2", [DEPTH, 64, 64])
    hy_b2 = din("hy_b2", [DEPTH, 64])
    hy_w3 = din("hy_w3", [DEPTH, 64, 1024])
    hy_d = din("hy_d", [DEPTH, 512])
    final_g = din("final_g", [D])
    c_idf = din("c_idf", [128, 128])
    c_idb = din("c_idb", [128, 128], BF16)
    c_rt = din("c_rt", [128, 128], BF16)
    c_cos = din("c_cos", [128, LS], BF16)
    c_sin = din("c_sin", [128, LS], BF16)
    c_z = [din("c_z0", [33, LP]), din("c_z1", [33, LS])]
    c_dec = [din("c_dec0", [LP, 512]), din("c_dec1", [LS, 512])]
    c_F = [din("c_F0", [LP, 2 * LP], BF16), din("c_F1", [LS, 2 * LS], BF16)]
    c_Fi = [din("c_Fi0", [2 * LP, LP], BF16), din("c_Fi1", [2 * LS, LS], BF16)]
    yout = [dout("yp", [TOK, D]), dout("ys", [TOK, D])]
    nk = dout("nk", [4, DEPTH, LP, 1024])
    nv = dout("nv", [4, DEPTH, LP, 1024])
    nst = dout("nst", [4, DEPTH, 2, 512])
    dbg_outs = {}
    gscr = [[nc.dram_tensor(f"gscr{g_}{l_}", [128, (2 * (LP if g_ == 0 else LS)) // 128, 512], BF16, kind="Internal").ap()
             for l_ in range(DEPTH)] for g_ in range(2)]

    def sb(name, shape, dt=F32):
        return nc.alloc_sbuf_tensor(name, list(shape), dt).ap()

    uid = [0]

    def un(name):
        uid[0] += 1
        return f"{name}_{uid[0]}"

    XT = sb("XT", [128, 16, TOK], F32)
    HT = sb("HT", [128, 16, TOK], BF16)
    MT = sb("MT", [128, 16, TOK], BF16)
    WS = [sb(f"WS{i}", [128, 16, 512], BF16) for i in range(2)]
    bXT = [k.buf("xt") for _ in range(16)]
    bHT = [k.buf("ht") for _ in range(16)]
    bMT = [k.buf("mt") for _ in range(16)]
    bWS = [k.buf("ws") for _ in range(2)]
    IDF = sb("IDF", [128, 128], F32)
    IDB = sb("IDB", [128, 128], BF16)
    ONES = sb("ONES", [128, 128], BF16)
    bCONST = k.buf("const")
    GVT = sb("GVT", [128, 80], F32)
    bGVT = k.buf("gvt")
    PVT = [sb(f"PVT{l}", [128, 104], F32) for l in range(DEPTH)]
    bPVT = [k.buf("pvt") for _ in range(DEPTH)]
    MODT = [sb(f"MODT{l}", [128, 48, 2], F32) for l in range(DEPTH)]
    bMODT = [k.buf("modt") for _ in range(DEPTH)]
    SMALL = sb("SMALL", [128, 64], F32)
    bSMALL = k.buf("small")
    LAMS = sb("LAMS", [128, 8], F32)
    bLAMS = k.buf("lams")
    PS = [nc.alloc_psum_tensor(f"PS{i}", [128, 512], F32).ap() for i in range(8)]
    bPS = [k.buf("ps") for _ in range(8)]
    ring = {"A": [0, 1, 2, 3], "B": [4, 5, 6, 7], "ALL": list(range(8)), "G": [4, 5, 6], "O": [0, 1, 2, 3, 7]}
    rpos = {"A": 0, "B": 0, "ALL": 0, "G": 0, "O": 0}

    def psum(rname="A"):
        lst = ring[rname]
        i = lst[rpos[rname] % len(lst)]
        rpos[rname] += 1
        return PS[i], bPS[i]

    def dump(name, ap, b, shape, dt=F32):
        if debug is None or name not in debug:
            return
        o = dout("dbg_" + name, shape, dt)
        dbg_outs[name] = o
        bb = b if isinstance(b, (list, tuple)) else [b]
        for x in bb[:-1]:
            deps = {}
            k._acc(deps, x.w)
            k._wait(SP, deps)
        k.dma_out(SP, bb[-1], o, ap)

    def mm(out, lhsT, rhs, start, stop, reads, writes, inc=None):
        if inc is None:
            inc = stop
        k.op(PE, lambda: nc.tensor.matmul(out, lhsT, rhs, start=start, stop=stop), reads, writes, inc)

    def tr(out, in_, ident, reads, writes, inc=True):
        k.op(PE, lambda: nc.tensor.transpose(out, in_, ident), reads, writes, inc)

    def act(out, in_, func, reads, writes, bias=None, scale=None, accum_out=None):
        kw = {}
        if bias is not None:
            kw["bias"] = bias
        if scale is not None:
            kw["scale"] = scale
        if accum_out is not None:
            kw["accum_out"] = accum_out
        k.op(ACT, lambda: nc.scalar.activation(out=out, in_=in_, func=func, **kw), reads, writes)

    def tt(E, out, a, b, op_, reads, writes):
        k.op(E, lambda: E.e.tensor_tensor(out, a, b, op_), reads, writes)

    def ts(E, out, a, s1, s2, op0, op1, reads, writes):
        if op1 is None:
            k.op(E, lambda: E.e.tensor_scalar(out, a, s1, None, op0), reads, writes)
        else:
            k.op(E, lambda: E.e.tensor_scalar(out, a, s1, s2, op0, op1), reads, writes)

    def stt(out, a, s, b, op0, op1, reads, writes):
        k.op(DVE, lambda: nc.vector.scalar_tensor_tensor(out, a, s, b, op0, op1), reads, writes)

    def cp(E, out, in_, reads, writes):
        if E is ACT:
            act(out, in_, AF.Copy, reads, writes)
        else:
            k.op(E, lambda: E.e.tensor_copy(out, in_), reads, writes)

    def wsrc_cols(w, l, c0, n=512):
        return w[l].rearrange("(kc p) n -> p kc n", p=128)[:, :, c0:c0 + n]

    items = []

    def add_item(label, src, shape, cast):
        items.append((label, src, shape, cast))

    for l in range(DEPTH):
        for t in range(12):
            add_item(("ada", l, t), wsrc_cols(w_ada, l, 512 * t), (16, 512), True)

    def pass_items(l, g):
        L = LP if g == 0 else LS
        gi = 0 if g == 0 else 1
        for blk in range(2):
            add_item(("q", l, g, blk), wsrc_cols(w_in, l, 0 + 512 * blk), (16, 512), True)
            add_item(("k", l, g, blk), wsrc_cols(w_in, l, 1024 + 512 * blk), (16, 512), True)
            add_item(("v", l, g, blk), wsrc_cols(w_in, l, 2048 + 512 * blk), (16, 512), True)
        for blk in range(2):
            add_item(("ag", l, g, blk), wsrc_cols(w_in, l, 3072 + 512 * blk), (16, 512), True)
        add_item(("lx", l, g), wsrc_cols(w_in, l, 4096), (16, 512), True)
        add_item(("lg", l, g), wsrc_cols(w_in, l, 4608), (16, 512), True)
        T = L // 128
        npiece = (2 * L) // 512
        Fv = c_F[gi].rearrange("(t p) f -> p t f", p=128)
        add_item(("hv", l, g), wsrc_cols(w_in, l, 5120), (16, 512), True)
        add_item(("hx1", l, g), wsrc_cols(w_in, l, 5632), (16, 512), True)
        for p in range(npiece):
            add_item(("Ff", l, g, p), Fv[:, :, 512 * p:512 * p + 512], (T, 512), False)
        Fiv = c_Fi[gi].rearrange("(kc p) t -> p kc t", p=128)
        if g == 0:
            add_item(("Fi", l, g, 0), Fiv, (4, 256), False)
        else:
            for h in range(2):
                add_item(("Fi", l, g, h), Fiv[:, :, 512 * h:512 * h + 512], (16, 512), False)
        add_item(("hx0", l, g), wsrc_cols(w_in, l, 6144), (16, 512), True)
        add_item(("hg", l, g), wsrc_cols(w_in, l, 6656), (16, 512), True)
        for t in range(4):
            add_item(("wo", l, g, t), wsrc_cols(w_out, l, 512 * t), (16, 512), True)

    for g in range(2):
        for l in range(DEPTH):
            pass_items(l, g)

    wstate = {"i": 0, "issued": 0}

    def w_issue(i):
        label, src, shape, cast = items[i]
        slot = i % 2
        dst = WS[slot][:, 0:shape[0], 0:shape[1]]
        Q = POOL if cast else SP
        k.dma_in(Q, bWS[slot], dst, src, persist=True)

    def wprefetch():
        i = wstate["i"]
        while wstate["issued"] <= min(i, len(items) - 1):
            w_issue(wstate["issued"])
            wstate["issued"] += 1

    def wnext(label, prefetch=True):
        i = wstate["i"]
        assert items[i][0] == label, (items[i][0], label)
        while wstate["issued"] <= min(i + (1 if prefetch else 0), len(items) - 1):
            w_issue(wstate["issued"])
            wstate["issued"] += 1
        wstate["i"] += 1
        shape = items[i][2]
        return WS[i % 2], bWS[i % 2]

    k.dma_in(SP, bCONST, IDF, c_idf, persist=True)
    k.dma_in_more(SP, bCONST, IDB, c_idb)
    k.op(DVE, lambda: nc.vector.memset(ONES, 1.0), [], [bCONST])

    with nc.sbuf_tensor(un("GV"), [128, 128], F32) as GVh, nc.sbuf_tensor(un("PV0"), [128, 128], F32) as PV0h, \
            nc.sbuf_tensor(un("PV1"), [128, 128], F32) as PV1h, nc.sbuf_tensor(un("LQ"), [128, 4, DEPTH, 64], F32) as LQh, \
            nc.sbuf_tensor(un("LT"), [128, DEPTH, 64], F32) as LTh:
        GV = GVh[:]
        PV = [PV0h[:], PV1h[:]]
        LQ = LQh[:]
        LT = LTh[:]
        bGV = k.buf("gv")
        k.op(DVE, lambda: nc.vector.memset(GV, 0.0), [], [bGV])
        k.dma_in(SP, bGV, GV[0:32, :], cvec.rearrange("r (kc p) -> (r kc) p", p=128))
        k.dma_in_more(SP, bGV, GV[32:64, :], norm_g.rearrange("l (kc p) -> (l kc) p", p=128))
        k.dma_in_more(SP, bGV, GV[64:80, :], final_g.rearrange("(kc p) -> kc p", p=128))
        ps, bps = psum()
        tr(ps[:, 0:128], GV, IDF, [bGV, bCONST], [bps])
        cp(DVE, GVT, ps[:, 0:80], [bps], [bGVT])
        for l in range(DEPTH):
            bPV = k.buf("pv")
            k.op(DVE, lambda: nc.vector.memset(PV[l], 0.0), [], [bPV])
            P_ = PV[l]
            k.dma_in(SP, bPV, P_[R_LCW:R_LCW + 16, :], lru_conv_w[l].rearrange("k (j p) -> (k j) p", p=128))
            k.dma_in_more(SP, bPV, P_[R_LCB:R_LCB + 4, :], lru_conv_b[l].rearrange("(j p) -> j p", p=128))
            k.dma_in_more(SP, bPV, P_[R_LBA:R_LBA + 8, :], lru_ba[l].rearrange("d (j p) -> (d j) p", p=128))
            k.dma_in_more(SP, bPV, P_[R_LBI:R_LBI + 8, :], lru_bi[l].rearrange("d (j p) -> (d j) p", p=128))
            k.dma_in_more(SP, bPV, P_[R_LLAM:R_LLAM + 8, :], lru_lam[l].rearrange("d (j p) -> (d j) p", p=128))
            k.dma_in_more(SP, bPV, P_[R_HCW:R_HCW + 36, :], hy_conv_w[l].rearrange("k (m p) -> (k m) p", p=128))
            k.dma_in_more(SP, bPV, P_[R_HCB:R_HCB + 12, :], hy_conv_b[l].rearrange("(m p) -> m p", p=128))
            k.dma_in_more(SP, bPV, P_[R_ST:R_ST + 8, :], st_lru[l].rearrange("d (j p) -> (d j) p", p=128))
            k.dma_in_more(SP, bPV, P_[R_SUB:R_SUB + 1, :], subln[l:l + 1, :])
            k.dma_in_more(SP, bPV, P_[R_B1:R_B1 + 1, 0:64], hy_b1[l:l + 1, :])
            k.dma_in_more(SP, bPV, P_[R_B2:R_B2 + 1, 0:64], hy_b2[l:l + 1, :])
            ps, bps = psum()
            tr(ps[:, 0:128], P_, IDF, [bPV, bCONST], [bps])
            cp(DVE, PVT[l], ps[:, 0:104], [bps], [bPVT[l]])
        bLQ = k.buf("lq")
        for i in range(4):
            for l in range(DEPTH):
                if i == 0 and l == 0:
                    k.dma_in(SP, bLQ, LQ[:, i, l, :], lamv[i][l].partition_broadcast(128))
                else:
                    k.dma_in_more(SP, bLQ, LQ[:, i, l, :], lamv[i][l].partition_broadcast(128))
        bLT = k.buf("lt")
        tt(DVE, LT, LQ[:, 0], LQ[:, 1], ALU.mult, [bLQ], [bLT])
        k.op(DVE, lambda: nc.vector.reduce_sum(LAMS[:, 0:2], LT, mybir.AxisListType.X), [bLT], [bLAMS])
        tt(DVE, LT, LQ[:, 2], LQ[:, 3], ALU.mult, [bLQ], [bLT])
        k.op(DVE, lambda: nc.vector.reduce_sum(LAMS[:, 2:4], LT, mybir.AxisListType.X), [bLT], [bLAMS])
        act(LAMS[:, 0:4], LAMS[:, 0:4], AF.Exp, [bLAMS], [bLAMS])
        tt(DVE, LAMS[:, 0:2], LAMS[:, 0:2], LAMS[:, 2:4], ALU.subtract, [bLAMS], [bLAMS])
        for l in range(DEPTH):
            lam_init = 0.8 - 0.6 * math.exp(-0.3 * l)
            ts(DVE, LAMS[:, 4 + l:5 + l], LAMS[:, l:l + 1], lam_init, -1.0, ALU.add, ALU.mult, [bLAMS], [bLAMS])
        k.barrier()

    def gen_G(g, l):
        with nc.sbuf_tensor(un("gZT"), [33, 512], F32) as ZTh, nc.sbuf_tensor(un("gHW1"), [33, 64], F32) as HW1h, \
                nc.sbuf_tensor(un("gHW2"), [64, 64], F32) as HW2h, nc.sbuf_tensor(un("gHW3"), [64, 1024], BF16) as HW3h, \
                nc.sbuf_tensor(un("gH1"), [64, 512], F32) as H1h, nc.sbuf_tensor(un("gH2"), [64, LS], BF16) as H2h, \
                nc.sbuf_tensor(un("gDEC"), [128, 2, 512], F32) as DECh, nc.sbuf_tensor(un("gDROW"), [1, 512], F32) as DRh, \
                nc.sbuf_tensor(un("gFT"), [128, 2, 512], F32) as FTh, \
                nc.sbuf_tensor(un("gGST"), [128, 2, 512], BF16) as GSTh:
            ZT, HW1, HW2, HW3, H1, H2, DEC, DROW, FT, GST = (ZTh[:], HW1h[:], HW2h[:], HW3h[:], H1h[:], H2h[:],
                                                            DECh[:], DRh[:], FTh[:], GSTh[:])
            FB = HT[:, 0:8, :].rearrange("p (a b) (c d) -> p a (b c) d", a=2, d=512)
            WP = HT[:, 8:12, :].rearrange("p a (b c) -> p (a b) c", c=512)
            WM = HT[:, 12:16, :].rearrange("p a (b c) -> p (a b) c", c=512)
            bF = k.buf("filt")
            bF3 = k.buf("filt3")
            bZT = k.buf("zt")
            bH1 = k.buf("h1")
            bH2 = k.buf("h2")
            bDEC = [k.buf("dec"), k.buf("dec")]
            bWPM = k.buf("wpm")
            bFT = [k.buf("ft"), k.buf("ft")]
            bFB = [k.buf("fb"), k.buf("fb")]
            bGST = [k.buf("gst"), k.buf("gst")]
            k.merge_into(bFB + [bWPM], bHT)
            gn = [0]
            fn = [0]
            if True:
                L = LP if g == 0 else LS
                T = L // 128
                npiece = (2 * L) // 512
                Fv = c_F[g].rearrange("(t p) f -> p t f", p=128)
                decv = c_dec[g].rearrange("(t p) c -> p t c", p=128)
                if True:
                    pv = PVT[l]
                    k.dma_in(SP, bF, HW1, hy_w1[l])
                    k.dma_in_more(SP, bF, HW2, hy_w2[l])
                    k.dma_in_more(SP, bF, DROW, hy_d[l:l + 1, :])
                    k.dma_in(POOL, bF3, HW3, hy_w3[l])
                    nch = max(1, L // 512)
                    cw_ = min(L, 512)
                    for ci in range(nch):
                        k.dma_in(SP, bZT, ZT[:, 0:cw_], c_z[g][:, cw_ * ci:cw_ * ci + cw_])
                        ps, bps = psum("G")
                        mm(ps[0:64, 0:cw_], HW1, ZT[:, 0:cw_], True, True, [bF, bZT], [bps])
                        wrap_sin(H1[:, 0:cw_], bH1, ps[0:64, 0:cw_], bps, pv[0:64, R_B1:R_B1 + 1],
                                 FT[0:64, 0, 0:cw_], FT[0:64, 1, 0:cw_], bFT[0], bFT[1], [bPVT[l]])
                        yield
                        ps, bps = psum("G")
                        mm(ps[0:64, 0:cw_], HW2, H1[:, 0:cw_], True, True, [bF, bH1], [bps])
                        wrap_sin(H2[:, cw_ * ci:cw_ * ci + cw_], bH2, ps[0:64, 0:cw_], bps, pv[0:64, R_B2:R_B2 + 1],
                                 FT[0:64, 0, 0:cw_], FT[0:64, 1, 0:cw_], bFT[0], bFT[1], [bPVT[l]])
                        yield
                    k.dma_in(SP, bDEC[0], DEC[:, 0, :], decv[:, 0, :])
                    for t in range(T):
                        r = t % 2
                        if t + 1 < T:
                            k.dma_in(SP, bDEC[1 - r], DEC[:, 1 - r, :], decv[:, t + 1, :])
                        psf, bpsf = psum("G")
                        psb_, bpsb = psum("G")
                        mm(psf, H2[:, 128 * t:128 * t + 128], HW3[:, 0:512], True, True, [bH2, bF3], [bpsf])
                        mm(psb_, H2[:, 128 * t:128 * t + 128], HW3[:, 512:1024], True, True, [bH2, bF3], [bpsb])
                        hf = FT[:, 0, :]
                        hb = FT[:, 1, :]
                        tt(DVE, hf, psf, DEC[:, r, :], ALU.mult, [bpsf, bDEC[r]], [bFT[0]])
                        tt(DVE, hb, psb_, DEC[:, r, :], ALU.mult, [bpsb, bDEC[r]], [bFT[1]])
                        if t == 0:
                            k.op(DVE, lambda: nc.vector.memset(hb[0:1, :], 0.0), [], [bFT[1]])
                            tt(DVE, hf[0:1, :], hf[0:1, :], DROW, ALU.add, [bFT[0], bF], [bFT[0]])
                        tt(DVE, WP[:, t, :], hf, hb, ALU.add, [bFT[0], bFT[1]], [bWPM])
                        tt(DVE, WM[:, t, :], hf, hb, ALU.subtract, [bFT[0], bFT[1]], [bWPM])
                        yield
                    k.dma_in(SP, bFB[fn[0] % 2], FB[:, fn[0] % 2, 0:T, :], Fv[:, :, 0:512])
                    for p in range(npiece):
                        fr = fn[0] % 2
                        fn[0] += 1
                        if p + 1 < npiece:
                            k.dma_in(SP, bFB[1 - fr], FB[:, 1 - fr, 0:T, :], Fv[:, :, 512 * (p + 1):512 * (p + 1) + 512])
                        for q in range(4):
                            kc = 4 * p + q
                            src = WP if q < 2 else WM
                            ps, bps = psum("G")
                            for t in range(T):
                                mm(ps, FB[:, fr, t, 128 * q:128 * q + 128], src[:, t, :], t == 0, t == T - 1,
                                   [bFB[fr], bWPM], [bps])
                            r = gn[0] % 2
                            gn[0] += 1
                            cp(ACT, GST[:, r, :], ps, [bps], [bGST[r]])
                            k.dma_out(SP, bGST[r], gscr[g][l][:, kc, :], GST[:, r, :])
                            yield
            k.merge_into(bHT, bFB + [bWPM])
            k.gen_bufs = [bF, bF3, bZT, bH1, bH2, bWPM] + bDEC + bFT + bFB + bGST

    def phase_ada():
        with nc.sbuf_tensor(un("SC"), [128, 16, 2], BF16) as SCh, nc.sbuf_tensor(un("MODROW"), [2, 2, 512], F32) as MRh, \
                nc.sbuf_tensor(un("BADA"), [2, 2, 512], F32) as BAh:
            SC = SCh[:]
            MR = MRh[:]
            BA = BAh[:]
            bSC = k.buf("sc")
            ggen = gen_G(0, 0)
            bMR = [k.buf("mr"), k.buf("mr")]
            bBA = [k.buf("ba"), k.buf("ba")]
            for r in range(2):
                act(SC[:, :, r], GVT[:, 16 * r:16 * r + 16], AF.Silu, [bGVT], [bSC])
            def first_pass_prologue():
                yield from load_x_gen(0, "G")
                yield from phase_norm_gen(0, 0, "G")

            fpp = None
            for l in range(DEPTH):
                psR, bpsR = PS[7], bPS[7]
                if l == 1:
                    for _ in ggen:
                        pass
                    fpp = first_pass_prologue()
                for t in range(12):
                    r = t % 2
                    k.dma_in(SP, bBA[r], BA[:, r, :], b_ada[l, 512 * t:512 * t + 512].partition_broadcast(2))
                    W, bW = wnext(("ada", l, t))
                    ps, bps = psum()
                    for kc in range(16):
                        mm(ps[0:2, :], SC[:, kc, :], W[:, kc, :], kc == 0, kc == 15, [bSC, bW], [bps])
                    tt(DVE, MR[:, r, :], ps[0:2, :], BA[:, r, :], ALU.add, [bps, bBA[r]], [bMR[r]])
                    for j4 in range(4):
                        j = 4 * t + j4
                        mm(psR[:, 2 * j:2 * j + 2], MR[0:2, r, 128 * j4:128 * j4 + 128], IDF[0:2, 0:2], True, True,
                           [bMR[r], bCONST], [bpsR], inc=True)
                    for _ in range(4):
                        next(ggen, None)
                    if fpp is not None:
                        for _ in range(2):
                            next(fpp, None)
                cp(DVE, MODT[l].rearrange("p a b -> p (a b)"), psR[:, 0:96], [bpsR], [bMODT[l]])
            for _ in ggen:
                pass
            if fpp is not None:
                for _ in fpp:
                    pass
            k.barrier()

    def load_x(g):
        for _ in load_x_gen(g):
            pass

    def load_x_gen(g, rname="ALL"):
        nst_ = 2 if rname == "G" else 3
        with nc.sbuf_tensor(un("XST"), [128, nst_, D], F32) as s0:
            st = [s0[:][:, q_, :] for q_ in range(nst_)]
            bst = [k.buf("xst") for _ in range(nst_)]
            for i in range(8):
                s = st[i % nst_]
                bs = bst[i % nst_]
                k.dma_in(SP, bs, s, xin[g][128 * i:128 * i + 128, :])
                for q4 in range(4):
                    ps, bps = psum(rname)
                    for j in range(4):
                        dc = 4 * q4 + j
                        tr(ps[:, 128 * j:128 * j + 128], s[:, 128 * dc:128 * dc + 128], IDF, [bs, bCONST], [bps],
                           inc=(j == 3))
                    E = ACT if q4 % 2 == 0 else DVE
                    cp(E, XT[:, 4 * q4:4 * q4 + 4, 128 * i:128 * i + 128],
                       ps.rearrange("p (a b) -> p a b", a=4), [bps], bXT[4 * q4:4 * q4 + 4])
                yield
            k.soft_barrier()

    def norm_stats(RB, bRB):
        for _ in norm_stats_gen(RB, bRB):
            pass

    def norm_stats_gen(RB, bRB, rname="B"):
        with nc.sbuf_tensor(un("SQ"), [128, 4, 512], BF16) as SQh:
            SQ = SQh[:]
            bSQ = [k.buf("sq") for _ in range(4)]
            pss = [psum(rname) for _ in range(2)]
            n = 0
            for dc in range(16):
                for tc in range(2):
                    r = n % 4
                    n += 1
                    act(SQ[:, r, :], XT[:, dc, 512 * tc:512 * tc + 512], AF.Square, [bXT[dc]], [bSQ[r]])
                    mm(pss[tc][0], ONES, SQ[:, r, :], dc == 0, dc == 15, [bSQ[r], bCONST], [pss[tc][1]], inc=True)
                if dc % 4 == 3:
                    yield
            for tc in range(2):
                act(RB[:, 512 * tc:512 * tc + 512], pss[tc][0], AF.Ln, [pss[tc][1]], [bRB], scale=1.0 / D, bias=EPS_AP)
            act(RB, RB, AF.Exp, [bRB], [bRB], scale=-0.5)
            k.soft_barrier()

    def phase_norm(l, g):
        for _ in phase_norm_gen(l, g):
            pass

    def phase_norm_gen(l, g, rname="B"):
        with nc.sbuf_tensor(un("RB"), [128, TOK], F32) as RBh, nc.sbuf_tensor(un("NT"), [128, 2, TOK], F32) as NTh:
            RB = RBh[:]
            NT = NTh[:]
            bRB = k.buf("rb")
            bNT = [k.buf("nt"), k.buf("nt")]
            yield from norm_stats_gen(RB, bRB, rname)
            stt(SMALL[:, 0:16], MODT[l][:, 16:32, g], 1.0, GVT[:, 32 + 16 * l:48 + 16 * l], ALU.add, ALU.mult,
                [bMODT[l], bGVT], [bSMALL])
            for dc in range(16):
                r = dc % 2
                stt(NT[:, r, :], XT[:, dc, :], SMALL[:, dc:dc + 1], RB, ALU.mult, ALU.mult,
                    [bXT[dc], bSMALL, bRB], [bNT[r]])
                act(HT[:, dc, :], NT[:, r, :], AF.Identity, [bNT[r], bMODT[l]], [bHT[dc]],
                    bias=MODT[l][:, dc, g:g + 1])
                if dc % 2 == 1:
                    yield
            k.soft_barrier()

    def proj_fm(W, bW, j, tc, ps, bps, ntok=512):
        for kc in range(16):
            mm(ps, W[:, kc, 128 * j:128 * j + 128], HT[:, kc, 512 * tc:512 * tc + 512], kc == 0, kc == 15,
               [bW, bHT[kc]], [bps])

    def proj_tm(W, bW, i, ps, bps):
        for kc in range(16):
            mm(ps, HT[:, kc, 128 * i:128 * i + 128], W[:, kc, :], kc == 0, kc == 15, [bW, bHT[kc]], [bps])

    def phase_attn(l, g):
        nseq, L = (4, LP) if g == 0 else (1, LS)
        nctx = 0 if g == 0 else PAST
        KL = L + nctx
        KT_TOT = nseq * KL
        nkt_seq = KL // 128
        nq = 256 if g == 0 else 512
        nqc = L // nq
        lam_init = 0.8 - 0.6 * math.exp(-0.3 * l)
        NKST = 4 if g == 0 else 2
        V = MT[:, 8:14, :].rearrange("p a (b c) -> p (a b) c", c=512)
        if debug and debug.get("realV"):
            V = sb(un("Vreal"), [128, 8, 512], BF16)
        ROPE = MT[:, 14:16, :]
        with nc.sbuf_tensor(un("QT"), [128, 4, TOK], BF16) as QTh, nc.sbuf_tensor(un("KTt"), [128, 4, KT_TOT], BF16) as KTh, \
                nc.sbuf_tensor(un("ET"), [128, 4, 512], BF16) as ETh, nc.sbuf_tensor(un("KST"), [128, NKST, 512], F32 if g == 0 else BF16) as KSTh, \
                nc.sbuf_tensor(un("RAW"), [128, 2, 512], BF16) as RAWh, \
                nc.sbuf_tensor(un("AT"), [128, 3, 512], F32) as ATh, nc.sbuf_tensor(un("ROPT"), [128, 2, 512], BF16) as ROPTh, \
                nc.sbuf_tensor(un("RTM"), [128, 128], BF16) as RTMh, nc.sbuf_tensor(un("KCX"), [128, 4, 512], BF16) as KCXh, \
                nc.sbuf_tensor(un("SQA"), [128, 512], BF16) as SQAh, \
                nc.sbuf_tensor(un("QM"), [128, 2, 512 if g == 1 else 2], BF16) as QMh:
            QT, KT, ET, KST, RAW, AT, ROPT, RTM, KCX, SQA, QM = (QTh[:], KTh[:], ETh[:], KSTh[:], RAWh[:], ATh[:], ROPTh[:],
                                                               RTMh[:], KCXh[:], SQAh[:], QMh[:])
            bQM = [k.buf("qm"), k.buf("qm")]
            bQT = [k.buf("qt") for _ in range(4)]
            bKT = [k.buf("kt") for _ in range(4)]
            bV = k.buf("v")
            bET = [k.buf("et") for _ in range(4)]
            bKST = [k.buf("kst") for _ in range(NKST)]
            bRAW = [k.buf("raw") for _ in range(2)]
            bAT = [k.buf("at") for _ in range(3)]
            itn = [0]
            en = [0]
            tail = [None]
            qm_ready = [False]
            prestart = [None]
            bROPT = [k.buf("ropt") for _ in range(2)]
            bROPE = k.buf("rope")
            k.merge_into([bV], bMT[8:14])
            k.merge_into([bROPE], bMT[14:16])
            bKCX = [k.buf("kcx") for _ in range(4)]
            bSQA = k.buf("sqa")
            ts(DVE, SMALL[:, 33:34], PVT[l][:, R_SUB:R_SUB + 1], 1.0 - lam_init, None, ALU.mult, None, [bPVT[l]], [bSMALL])
            if g == 1:
                k.dma_in(SP, bROPE, ROPE[:, 0, :], c_cos)
                k.dma_in_more(SP, bROPE, ROPE[:, 1, :], c_sin)
                k.dma_in_more(SP, bROPE, RTM, c_rt)

            def rope_apply(src, rd_src, dst, bdst, tok0):
                ps2, bps2 = psum("A")
                mm(ps2, RTM, src, True, True, [bROPE] + rd_src, [bps2])
                a0 = ROPT[:, 0, :]
                a1 = ROPT[:, 1, :]
                tt(DVE, a0, src, ROPE[:, 0, tok0:tok0 + 512], ALU.mult, rd_src + [bROPE], [bROPT[0]])
                tt(DVE, a1, ps2, ROPE[:, 1, tok0:tok0 + 512], ALU.mult, [bps2, bROPE], [bROPT[1]])
                tt(DVE, dst, a0, a1, ALU.add, [bROPT[0], bROPT[1]], [bdst])

            rawn = [0]
            for blk in range(2):
                if g == 1:
                    ckv = ck[l].rearrange("(t p) c -> p t c", p=128)
                    for t in range(4):
                        k.dma_in(POOL, bKCX[t], KCX[:, t, :], ckv[:, t, 512 * blk:512 * blk + 512])
                W, bW = wnext(("q", l, g, blk))
                for j in range(4):
                    for tc in range(2):
                        ps, bps = psum("A")
                        proj_fm(W, bW, j, tc, ps, bps)
                        if g == 0:
                            cp(ACT, QT[:, j, 512 * tc:512 * tc + 512], ps, [bps], [bQT[j]])
                        else:
                            r = rawn[0] % 2
                            rawn[0] += 1
                            cp(ACT, RAW[:, r, :], ps, [bps], [bRAW[r]])
                            rope_apply(RAW[:, r, :], [bRAW[r]], QT[:, j, 512 * tc:512 * tc + 512], bQT[j], 512 * tc)
                if debug and debug.get("sub") == "q":
                    k.barrier(recycle=False)
                    raise StopBuild()
                W, bW = wnext(("k", l, g, blk))
                if g == 1:
                    for t in range(4):
                        r = t
                        ps, bps = psum("A")
                        psb = ps.bitcast(BF16)
                        for j in range(4):
                            tr(psb[:, 128 * j:128 * j + 128], KCX[:, r, 128 * j:128 * j + 128], IDB, [bKCX[r], bCONST], [bps],
                               inc=(j == 3))
                        cp(DVE, KT[:, :, 128 * t:128 * t + 128], psb[:, 0:512].rearrange("p (a b) -> p a b", a=4),
                           [bps], bKT)
                pend_tr = [None]

                def k_transposes(i, r):
                    ps2, bps2 = psum("A")
                    if g == 0:
                        pv_, idn = ps2, IDF
                    else:
                        pv_, idn = ps2.bitcast(BF16), IDB
                    for j in range(4):
                        tr(pv_[:, 128 * j:128 * j + 128], KST[:, r, 128 * j:128 * j + 128], idn, [bKST[r], bCONST], [bps2],
                           inc=(j == 3))
                    cp(DVE, KT[:, :, nctx + 128 * i:nctx + 128 * i + 128], pv_[:, 0:512].rearrange("p (a b) -> p a b", a=4),
                       [bps2], bKT)

                for i in range(8):
                    ps, bps = psum("A")
                    proj_tm(W, bW, i, ps, bps)
                    if pend_tr[0] is not None:
                        k_transposes(*pend_tr[0])
                    r = i % NKST
                    cp(ACT, KST[:, r, :], ps, [bps], [bKST[r]])
                    if g == 0:
                        s_, t0 = divmod(128 * i, LP)
                        k.dma_out(SP, bKST[r], nk[s_, l, t0:t0 + 128, 512 * blk:512 * blk + 512], KST[:, r, :])
                    pend_tr[0] = (i, r)
                k_transposes(*pend_tr[0])
                if g == 1:
                    for j in range(4):
                        for tc in range(2):
                            src = KT[:, j, nctx + 512 * tc:nctx + 512 * tc + 512]
                            rope_apply(src, [bKT[j]], src, bKT[j], 512 * tc)
                if debug and debug.get("sub") == "k":
                    k.barrier(recycle=False)
                    raise StopBuild()
                W, bW = wnext(("v", l, g, blk))
                if g == 1:
                    k.dma_in(POOL, bV, V[:, 0:4, :], cv[l].rearrange("(t p) c -> p t c", p=128)[:, :, 512 * blk:512 * blk + 512])
                for i in range(8):
                    ps, bps = psum("A")
                    proj_tm(W, bW, i, ps, bps)
                    vt = (nctx // 128) + i
                    if g == 0:
                        r = i % NKST
                        cp(ACT, KST[:, r, :], ps, [bps], [bKST[r]])
                        cp(DVE, V[:, vt, :], KST[:, r, :], [bKST[r]], [bV])
                        s_, t0 = divmod(128 * i, LP)
                        k.dma_out(SP, bKST[r], nv[s_, l, t0:t0 + 128, 512 * blk:512 * blk + 512], KST[:, r, :])
                    else:
                        cp(DVE, V[:, vt, :], ps, [bps], [bV])
                if debug and debug.get("sub") == "v":
                    k.barrier(recycle=False)
                    raise StopBuild()
                if g == 0:
                    iters = [[(s2 * LP, s2 * KL, s2 * nkt_seq) for s2 in (sp, sp + 1)] for sp in (0, 2)]
                    nqs = 256
                else:
                    iters = [[(qc * 512, 0, 0)] for qc in range(2)]
                    nqs = 512
                for j in range(4):
                    h = 4 * blk + j
                    for subs in iters:
                        n_it = itn[0]
                        itn[0] += 1
                        qbase = subs[0][0]
                        Bb_ = AT[:, 1 + n_it % 2, :]
                        bBb = bAT[1 + n_it % 2]
                        Aa_ = AT[:, 0, :]
                        bAa = bAT[0]
                        accs = [psum("B") for _ in range(4)]
                        for c in range(2):
                            Ops, bO = accs[2 * c]
                            Zps, bZ = accs[2 * c + 1]
                            pend = []
                            if g == 1 and not qm_ready[0]:
                                ts(DVE, QM[:, c, :], QT[:, j, qbase:qbase + 512], CST[:, 2 + c:3 + c], None, ALU.mult, None,
                                   [bQT[j], bCST], [bQM[c]])
                            qm_ready[0] = False

                            def emit_s(kt):
                                ps, bps = psum("A")
                                for i_, (q0, k0b, vtb) in enumerate(subs):
                                    k0 = k0b + 128 * kt
                                    if g == 1:
                                        mm(ps[:, 0:512], KT[:, j, k0:k0 + 128], QM[:, c, :], True, True, [bKT[j], bQM[c]], [bps])
                                    else:
                                        mm(ps[:, i_ * nqs:(i_ + 1) * nqs], KT[64 * c:64 * c + 64, j, k0:k0 + 128],
                                           QT[64 * c:64 * c + 64, j, q0:q0 + nqs], True, True, [bKT[j], bQT[j]], [bps],
                                           inc=(i_ == len(subs) - 1))
                                e = en[0] % 4
                                en[0] += 1
                                act(ET[:, e, :], ps, AF.Exp, [bps], [bET[e]], scale=0.125)
                                pend.append((kt, e))

                            if len(subs) > 1:
                                if prestart[0] is not None:
                                    pend.extend(prestart[0])
                                    prestart[0] = None
                                else:
                                    for kt in range(nkt_seq):
                                        emit_s(kt)
                                for i_, (q0, k0b, vtb) in enumerate(subs):
                                    for kt_, e in pend:
                                        mm(Ops[:, i_ * nqs:(i_ + 1) * nqs], V[:, vtb + kt_, 128 * j:128 * j + 128],
                                           ET[:, e, i_ * nqs:(i_ + 1) * nqs], kt_ == 0, kt_ == nkt_seq - 1, [bV, bET[e]], [bO],
                                           inc=False)
                                for kt_, e in pend:
                                    mm(Zps, ONES, ET[:, e, :], kt_ == 0, kt_ == nkt_seq - 1, [bCONST, bET[e]], [bZ], inc=True)
                                if c == 0:
                                    nj, nsubs, ncc = j, subs, 1
                                else:
                                    idx = iters.index(subs)
                                    if idx + 1 < len(iters):
                                        nj, nsubs, ncc = j, iters[idx + 1], 0
                                    elif j + 1 < 4:
                                        nj, nsubs, ncc = j + 1, iters[0], 0
                                    else:
                                        nj = None
                                if nj is not None:
                                    pl = []
                                    for kt0 in range(nkt_seq):
                                        ps, bps = psum("A")
                                        for i_, (q0, k0b, vtb) in enumerate(nsubs):
                                            k0 = k0b + 128 * kt0
                                            mm(ps[:, i_ * nqs:(i_ + 1) * nqs], KT[64 * ncc:64 * ncc + 64, nj, k0:k0 + 128],
                                               QT[64 * ncc:64 * ncc + 64, nj, q0:q0 + nqs], True, True, [bKT[nj], bQT[nj]], [bps],
                                               inc=(i_ == len(nsubs) - 1))
                                        e = en[0] % 4
                                        en[0] += 1
                                        act(ET[:, e, :], ps, AF.Exp, [bps], [bET[e]], scale=0.125)
                                        pl.append((kt0, e))
                                    prestart[0] = pl
                            else:
                                if prestart[0] is not None:
                                    pend.extend(prestart[0])
                                    prestart[0] = None
                                else:
                                    emit_s(0)
                                    if nkt_seq > 1:
                                        emit_s(1)
                                for kt in range(nkt_seq):
                                    if kt + 2 < nkt_seq:
                                        emit_s(kt + 2)
                                    kt_, e = pend.pop(0)
                                    vt = subs[0][2] + kt_
                                    mm(Ops, V[:, vt, 128 * j:128 * j + 128], ET[:, e, :], kt_ == 0, kt_ == nkt_seq - 1,
                                       [bV, bET[e]], [bO], inc=False)
                                    mm(Zps, ONES, ET[:, e, :], kt_ == 0, kt_ == nkt_seq - 1, [bCONST, bET[e]], [bZ], inc=True)
                            if g == 1:
                                if c == 0:
                                    nj, nqb, ncc = j, qbase, 1
                                else:
                                    idx = iters.index(subs)
                                    if idx + 1 < len(iters):
                                        nj, nqb, ncc = j, iters[idx + 1][0][0], 0
                                    elif j + 1 < 4:
                                        nj, nqb, ncc = j + 1, iters[0][0][0], 0
                                    else:
                                        nj = None
                                if nj is not None:
                                    ts(DVE, QM[:, ncc, :], QT[:, nj, nqb:nqb + 512], CST[:, 2 + ncc:3 + ncc], None, ALU.mult, None,
                                       [bQT[nj], bCST], [bQM[ncc]])
                                    qm_ready[0] = True
                                    pl = []
                                    for kt0 in range(2):
                                        ps, bps = psum("A")
                                        mm(ps[:, 0:512], KT[:, nj, 128 * kt0:128 * kt0 + 128], QM[:, ncc, :], True, True,
                                           [bKT[nj], bQM[ncc]], [bps])
                                        e = en[0] % 4
                                        en[0] += 1
                                        act(ET[:, e, :], ps, AF.Exp, [bps], [bET[e]], scale=0.125)
                                        pl.append((kt0, e))
                                    prestart[0] = pl
                            if c == 0 and tail[0] is not None:
                                tail[0]()
                                tail[0] = None
                            act(Aa_, Zps, AF.Ln, [bZ], [bAa])
                            act(Aa_, Aa_, AF.Exp, [bAa], [bAa], scale=-1.0)
                            if c == 0:
                                tt(DVE, Bb_, Ops, Aa_, ALU.mult, [bO, bAa], [bBb])
                            else:
                                tt(DVE, Aa_, Ops, Aa_, ALU.mult, [bO, bAa], [bAa])
                                stt(Bb_, Aa_, LAMS[:, 4 + l:5 + l], Bb_, ALU.mult, ALU.add, [bAa, bBb, bLAMS], [bBb])

                        def make_tail(Bb_=Bb_, bBb=bBb, h=h, qbase=qbase):
                            def _t():
                                tt(DVE, SQA, Bb_, Bb_, ALU.mult, [bBb], [bSQA])
                                ps, bps = psum("A")
                                mm(ps, ONES, SQA, True, True, [bCONST, bSQA], [bps])
                                act(ps, ps, AF.Ln, [bps], [bps], scale=1.0 / 128.0, bias=EPS_AP)
                                act(ps, ps, AF.Exp, [bps], [bps], scale=-0.5)
                                stt(MT[:, h, qbase:qbase + 512], Bb_, SMALL[:, 33:34], ps, ALU.mult, ALU.mult,
                                    [bBb, bSMALL, bps], [bMT[h]])
                            return _t
                        tail[0] = make_tail()
            if tail[0] is not None:
                tail[0]()
                tail[0] = None
            if debug and debug.get("sub") == "heads":
                k.barrier(recycle=False)
                raise StopBuild()
            for blk in range(2):
                W, bW = wnext(("ag", l, g, blk))
                for j in range(4):
                    h = 4 * blk + j
                    for tc in range(2):
                        ps, bps = psum("A")
                        proj_fm(W, bW, j, tc, ps, bps)
                        r = tc
                        act(ET[:, r, :], ps, AF.Silu, [bps], [bET[r]])
                        sl = MT[:, h, 512 * tc:512 * tc + 512]
                        tt(DVE, sl, sl, ET[:, r, :], ALU.mult, [bMT[h], bET[r]], [bMT[h]])
            k.merge_into(bMT[8:14], [bV])
            k.merge_into(bMT[14:16], [bROPE])
            k.soft_barrier()

    def phase_lru(l, g):
        nseq, L = (4, LP) if g == 0 else (1, LS)
        with nc.sbuf_tensor(un("WBD"), [128, 2, 4, 128], F32) as WBDh, \
                nc.sbuf_tensor(un("LW"), [128, 7, TOK], F32) as LWh, nc.sbuf_tensor(un("LX2"), [128, TOK], F32) as LX2h, \
                nc.sbuf_tensor(un("SGb"), [128, TOK], BF16) as SGbh, nc.sbuf_tensor(un("XC2"), [128, TOK], F32) as XC2h:
            WBD, LW, LX2, SGb, XC2 = WBDh[:], LWh[:], LX2h[:], SGbh[:], XC2h[:]
            HL, HLT, bHL, bHLT = HL_P, HLT_P, bHL_P, bHLT_P
            bWBDp = [k.buf("wbd"), k.buf("wbd")]
            bLW = [k.buf("lw") for _ in range(8)]
            bLX2 = k.buf("lx2")
            bXC2 = k.buf("xc2")
            XCs = [LW[:, 0, :], XC2]
            bXCs = [bLW[0], bXC2]
            k.op(DVE, lambda: nc.vector.memset(WBD[:, 0], 0.0), [], [bWBDp[0]])
            k.op(DVE, lambda: nc.vector.memset(WBD[:, 1], 0.0), [], [bWBDp[1]])

            def load_wbd(j):
                first = True
                for gate, wsrc in enumerate((lru_wa, lru_wi)):
                    for d in range(2):
                        for half in range(2):
                            dst = WBD[64 * half:64 * half + 64, j % 2, gate * 2 + d, 64 * half:64 * half + 64]
                            src = wsrc[l, d, 2 * j + half]
                            if first:
                                k.dma_in(SP, bWBDp[j % 2], dst, src)
                                first = False
                            else:
                                k.dma_in_more(SP, bWBDp[j % 2], dst, src)

            load_wbd(0)
            load_wbd(1)
            act(SMALL[:, 16:24], PVT[l][:, R_LLAM:R_LLAM + 8], AF.Exp, [bPVT[l]], [bSMALL], scale=-1.0)
            act(SMALL[:, 16:24], SMALL[:, 16:24], AF.Ln, [bSMALL], [bSMALL], bias=ONE_AP)
            ts(DVE, SMALL[:, 24:32], SMALL[:, 16:24], -16.0, None, ALU.mult, None, [bSMALL], [bSMALL])
            ts(DVE, SMALL[:, 16:24], SMALL[:, 16:24], -8.0, None, ALU.mult, None, [bSMALL], [bSMALL])
            Wx, bWx = wnext(("lx", l, g))
            Wg, bWg = wnext(("lg", l, g), prefetch=False)
            pv = PVT[l]
            def lx_proj(j):
                for tc in range(2):
                    ps, bps = psum("A")
                    proj_fm(Wx, bWx, j, tc, ps, bps)
                    cp(ACT, LX2[:, 512 * tc:512 * tc + 512], ps, [bps], [bLX2])

            def conv(j):
                XCj, bXCj = XCs[j % 2], bXCs[j % 2]
                x3 = LX2.rearrange("p (s t) -> p s t", s=nseq)
                xc3 = XCj.rearrange("p (s t) -> p s t", s=nseq)

                def cw(kk):
                    c_ = R_LCW + 4 * kk + j
                    return pv[:, c_:c_ + 1]
                ts(DVE, XCj, LX2, cw(2), pv[:, R_LCB + j:R_LCB + j + 1], ALU.mult, ALU.add, [bLX2, bPVT[l]], [bXCj])
                stt(xc3[:, :, 2:L], x3[:, :, 0:L - 2], cw(0), xc3[:, :, 2:L], ALU.mult, ALU.add, [bLX2, bPVT[l], bXCj], [bXCj])
                stt(xc3[:, :, 1:L], x3[:, :, 0:L - 1], cw(1), xc3[:, :, 1:L], ALU.mult, ALU.add, [bLX2, bPVT[l], bXCj], [bXCj])
                stt(xc3[:, :, 0:L - 1], x3[:, :, 1:L], cw(3), xc3[:, :, 0:L - 1], ALU.mult, ALU.add, [bLX2, bPVT[l], bXCj], [bXCj])

            lx_proj(0)
            conv(0)
            for j in range(4):
                _, Rr, Ii, Aa, Bb, Hh, HS = [LW[:, i_, :] for i_ in range(7)]
                XC = XCs[j % 2]
                bLW[0] = bXCs[j % 2]
                bWBD = bWBDp[j % 2]
                SG = SGb
                for tc in range(2):
                    ps, bps = psum("A")
                    proj_fm(Wg, bWg, j, tc, ps, bps)
                    act(SG[:, 512 * tc:512 * tc + 512], ps, AF.Silu, [bps], [bLW[7]])
                if j + 1 < 4:
                    lx_proj(j + 1)
                    conv(j + 1)
                for d in range(2):
                    for tc in range(2):
                        ps, bps = psum("A")
                        mm(ps, WBD[:, j % 2, 0 * 2 + d, :], XC[:, 512 * tc:512 * tc + 512], True, True, [bWBD, bLW[0]], [bps])
                        act(Rr[:, 512 * tc:512 * tc + 512], ps, AF.Sigmoid, [bps, bPVT[l]], [bLW[1]],
                            bias=pv[:, R_LBA + 4 * d + j:R_LBA + 4 * d + j + 1])
                        ps, bps = psum("A")
                        mm(ps, WBD[:, j % 2, 1 * 2 + d, :], XC[:, 512 * tc:512 * tc + 512], True, True, [bWBD, bLW[0]], [bps])
                        act(Ii[:, 512 * tc:512 * tc + 512], ps, AF.Sigmoid, [bps, bPVT[l]], [bLW[2]],
                            bias=pv[:, R_LBI + 4 * d + j:R_LBI + 4 * d + j + 1])
                    cn = SMALL[:, 16 + 4 * d + j:17 + 4 * d + j]
                    cn2 = SMALL[:, 24 + 4 * d + j:25 + 4 * d + j]
                    act(Aa, Rr, AF.Exp, [bLW[1], bSMALL], [bLW[3]], scale=cn)
                    act(Bb, Rr, AF.Exp, [bLW[1], bSMALL], [bLW[4]], scale=cn2)
                    act(Bb, Bb, AF.Sqrt, [bLW[4]], [bLW[4]], scale=-1.0, bias=ONE_AP)
                    tt(DVE, Ii, Ii, XC, ALU.mult, [bLW[2], bLW[0]], [bLW[2]])
                    tt(DVE, Bb, Bb, Ii, ALU.mult, [bLW[4], bLW[2]], [bLW[4]])
                    for s_ in range(nseq):
                        sl = slice(s_ * L, s_ * L + L)
                        if g == 1:
                            init = pv[:, R_ST + 4 * d + j:R_ST + 4 * d + j + 1]
                        else:
                            init = 0.0
                        rd = [bLW[3], bLW[4]] + ([bPVT[l]] if g == 1 else [])
                        if d == 0:
                            k.op(DVE, lambda: nc.vector.tensor_tensor_scan(Hh[:, sl], Aa[:, sl], Bb[:, sl], init, ALU.mult, ALU.add),
                                 rd, [bLW[5]])
                        else:
                            k.op(DVE, lambda: nc.vector.tensor_tensor_scan(Hh[:, sl][:, ::-1], Aa[:, sl][:, ::-1], Bb[:, sl][:, ::-1],
                                                                          init, ALU.mult, ALU.add), rd, [bLW[5]])
                        if g == 0:
                            col = (s_ * 2 + d) * 4 + j
                            tcol = s_ * L + (L - 1 if d == 0 else 0)
                            cp(DVE, HL[:, col:col + 1], Hh[:, tcol:tcol + 1], [bLW[5]], [bHL])
                    if d == 0:
                        cp(DVE, HS, Hh, [bLW[5]], [bLW[6]])
                    else:
                        tt(DVE, HS, HS, Hh, ALU.add, [bLW[6], bLW[5]], [bLW[6]])
                tt(DVE, MT[:, 8 + j, :], HS, SG, ALU.mult, [bLW[6], bLW[7]], [bMT[8 + j]])
                if j + 2 < 4:
                    load_wbd(j + 2)
            wprefetch()
            if g == 0:
                def _hl_out():
                    ps, bps = psum("A")
                    tr(ps[0:32, 0:128], HL, IDF, [bHL, bCONST], [bps])
                    cp(DVE, HLT, ps[0:32, 0:128], [bps], [bHLT])
                    for s_ in range(4):
                        k.dma_out(SP, bHLT, nst[s_, l].rearrange("d (j p) -> (d j) p", p=128), HLT[8 * s_:8 * s_ + 8, :])
                deferred.append(_hl_out)
            k.soft_barrier()

    def wrap_sin(dst, bdst, ps, bps, bias_ap, x1, t1, bx1, bt1, extra_reads):
        ts(DVE, x1, ps, bias_ap, None, ALU.add, None, [bps] + extra_reads, [bx1])
        ts(DVE, t1, x1, PI, -2.0 * PI, ALU.is_gt, ALU.mult, [bx1], [bt1])
        tt(DVE, x1, x1, t1, ALU.add, [bx1, bt1], [bx1])
        ts(DVE, t1, x1, -PI, 2.0 * PI, ALU.is_lt, ALU.mult, [bx1], [bt1])
        tt(DVE, x1, x1, t1, ALU.add, [bx1, bt1], [bx1])
        ts(DVE, x1, x1, PI, -PI, ALU.min, ALU.max, [bx1], [bx1])
        act(dst, x1, AF.Sin, [bx1], [bdst])

    def phase_hyena(l, g):
        nseq, L = (4, LP) if g == 0 else (1, LS)
        gi = g
        T = L // 128
        npiece = (2 * L) // 512
        NKC = (2 * L) // 128
        pv = PVT[l]

        def conv3(dst, bdst, src, bsrc, m):
            def cw(kk):
                c_ = R_HCW + 12 * kk + m
                return pv[:, c_:c_ + 1]
            s3 = src.rearrange("p (s t) -> p s t", s=nseq)
            d3 = dst.rearrange("p (s t) -> p s t", s=nseq)
            act(dst, src, AF.Identity, [bsrc, bPVT[l]], [bdst], scale=cw(1), bias=pv[:, R_HCB + m:R_HCB + m + 1])
            stt(d3[:, :, 1:L], s3[:, :, 0:L - 1], cw(0), d3[:, :, 1:L], ALU.mult, ALU.add, [bsrc, bPVT[l], bdst], [bdst])
            stt(d3[:, :, 0:L - 1], s3[:, :, 1:L], cw(2), d3[:, :, 0:L - 1], ALU.mult, ALU.add, [bsrc, bPVT[l], bdst], [bdst])

        with nc.sbuf_tensor(un("G"), [128, NKC, 512], BF16) as Gh:
            G = Gh[:]
            bG = [k.buf("g") for _ in range(NKC)]
            bGall = k.buf("gall")
            k.dma_in(SP, bGall, G, gscr[g][l])
            for kc_ in range(NKC):
                bG[kc_].w = dict(bGall.w)
            with nc.sbuf_tensor(un("UT"), [128, TOK // 128, 512], BF16) as UTh:
                UT = UTh[:]
                bUT = k.buf("ut")
                with nc.sbuf_tensor(un("HW"), [128, 3, TOK], F32) as HWh, nc.sbuf_tensor(un("CV"), [128, 4, TOK], BF16) as CVh:
                    HW, CV = HWh[:], CVh[:]
                    bHW = [k.buf("hw") for _ in range(3)]
                    bCV = [k.buf("cv") for _ in range(4)]
                    W, bW = wnext(("hv", l, g))
                    for j in range(4):
                        rs = j % 2
                        for tc in range(2):
                            ps, bps = psum("A")
                            proj_fm(W, bW, j, tc, ps, bps)
                            cp(ACT, HW[:, rs, 512 * tc:512 * tc + 512], ps, [bps], [bHW[rs]])
                        conv3(HW[:, 2, :], bHW[2], HW[:, rs, :], bHW[rs], j)
                        cp(DVE, CV[:, j, :], HW[:, 2, :], [bHW[2]], [bCV[j]])
                    for fn_ in deferred:
                        fn_()
                    deferred.clear()
                    W, bW = wnext(("hx1", l, g))

                    def hx1_proj(j):
                        rs = j % 2
                        for tc in range(2):
                            ps, bps = psum("A")
                            proj_fm(W, bW, j, tc, ps, bps)
                            cp(ACT, HW[:, rs, 512 * tc:512 * tc + 512], ps, [bps], [bHW[rs]])

                    hx1_proj(0)
                    for j in range(4):
                        rs = j % 2
                        conv3(HW[:, 2, :], bHW[2], HW[:, rs, :], bHW[rs], 4 + j)
                        if j + 1 < 4:
                            hx1_proj(j + 1)
                        tt(DVE, CV[:, j, :], CV[:, j, :], HW[:, 2, :], ALU.mult, [bCV[j], bHW[2]], [bCV[j]])
                        for q2 in range(2):
                            ps, bps = psum("A")
                            psb = ps.bitcast(BF16)
                            for i4 in range(4):
                                i = 4 * q2 + i4
                                tr(psb[:, 128 * i4:128 * i4 + 128], CV[:, j, 128 * i:128 * i + 128], IDB, [bCV[j], bCONST], [bps],
                                   inc=(i4 == 3))
                            cp(ACT, UT[:, 4 * q2:4 * q2 + 4, 128 * j:128 * j + 128],
                               psb[:, 0:512].rearrange("p (a b) -> p a b", a=4), [bps], [bUT])
                    k.soft_barrier()
                npp = nseq * NKC if g == 0 else 1
                with nc.sbuf_tensor(un("PP"), [128, npp, 512], BF16) as PPh, nc.sbuf_tensor(un("TP"), [128, 4, 512], F32) as TPh:
                    PP, TP = PPh[:], TPh[:]
                    bPP = [k.buf("pp") for _ in range(npp)]
                    bTP = [k.buf("tp") for _ in range(4)]

                    def Pslot(s_, kc):
                        if g == 0:
                            return PP[:, s_ * NKC + kc, :], bPP[s_ * NKC + kc]
                        return G[:, kc, :], bG[kc]

                    for p in range(npiece):
                        W, bW = wnext(("Ff", l, g, p))
                        for s_ in range(nseq):
                            for q in range(2):
                                kcc, kcs = 4 * p + q, 4 * p + 2 + q
                                psc, bpsc = psum("A")
                                pss_, bpss = psum("A")
                                for t in range(T):
                                    mm(psc, W[:, t, 128 * q:128 * q + 128], UT[:, s_ * T + t, :], t == 0, t == T - 1, [bW, bUT], [bpsc])
                                for t in range(T):
                                    mm(pss_, W[:, t, 256 + 128 * q:256 + 128 * q + 128], UT[:, s_ * T + t, :], t == 0, t == T - 1,
                                       [bW, bUT], [bpss])
                                Gc, Gs = G[:, kcc, :], G[:, kcs, :]
                                tt(DVE, TP[:, 0, :], psc, Gc, ALU.mult, [bpsc, bG[kcc]], [bTP[0]])
                                tt(DVE, TP[:, 1, :], pss_, Gs, ALU.mult, [bpss, bG[kcs]], [bTP[1]])
                                tt(DVE, TP[:, 2, :], pss_, Gc, ALU.mult, [bpss, bG[kcc]], [bTP[2]])
                                tt(DVE, TP[:, 3, :], psc, Gs, ALU.mult, [bpsc, bG[kcs]], [bTP[3]])
                                Pc, bPc = Pslot(s_, kcc)
                                Ps_, bPs = Pslot(s_, kcs)
                                tt(DVE, Pc, TP[:, 0, :], TP[:, 1, :], ALU.subtract, [bTP[0], bTP[1]], [bPc])
                                tt(DVE, Ps_, TP[:, 2, :], TP[:, 3, :], ALU.add, [bTP[2], bTP[3]], [bPs])
                    if g == 0:
                        W, bW = wnext(("Fi", l, g, 0))
                        for s_ in range(nseq):
                            for j in range(4):
                                ps, bps = psum("A")
                                for kc in range(NKC):
                                    Pk, bPk = Pslot(s_, kc)
                                    mm(ps[:, 0:L], Pk[:, 128 * j:128 * j + 128], W[:, kc, 0:L], kc == 0, kc == NKC - 1, [bPk, bW], [bps])
                                cp(ACT, MT[:, 12 + j, s_ * L:s_ * L + L], ps[:, 0:L], [bps], [bMT[12 + j]])
                    else:
                        for hh in range(2):
                            W, bW = wnext(("Fi", l, g, hh))
                            for j in range(4):
                                ps, bps = psum("A")
                                for kc in range(NKC):
                                    Pk, bPk = Pslot(0, kc)
                                    mm(ps, Pk[:, 128 * j:128 * j + 128], W[:, kc, :], kc == 0, kc == NKC - 1, [bPk, bW], [bps])
                                cp(ACT, MT[:, 12 + j, 512 * hh:512 * hh + 512], ps, [bps], [bMT[12 + j]])
                    k.soft_barrier()
        with nc.sbuf_tensor(un("HW4"), [128, 4, TOK], F32) as HWh, nc.sbuf_tensor(un("SG4"), [128, 2, 512], BF16) as SGh:
            HW, SG = HWh[:], SGh[:]
            bHW = [k.buf("hw") for _ in range(4)]
            bSG = [k.buf("sg") for _ in range(2)]
            W, bW = wnext(("hx0", l, g))
            for j in range(4):
                for tc in range(2):
                    ps, bps = psum("A")
                    proj_fm(W, bW, j, tc, ps, bps)
                    cp(ACT, HW[:, j % 2, 512 * tc:512 * tc + 512], ps, [bps], [bHW[j % 2]])
                conv3(HW[:, 2 + j % 2, :], bHW[2 + j % 2], HW[:, j % 2, :], bHW[j % 2], 8 + j)
                tt(DVE, MT[:, 12 + j, :], MT[:, 12 + j, :], HW[:, 2 + j % 2, :], ALU.mult, [bMT[12 + j], bHW[2 + j % 2]],
                   [bMT[12 + j]])
            W, bW = wnext(("hg", l, g))
            for j in range(4):
                for tc in range(2):
                    ps, bps = psum("A")
                    proj_fm(W, bW, j, tc, ps, bps)
                    r = tc
                    act(SG[:, r, :], ps, AF.Silu, [bps], [bSG[r]])
                    sl = MT[:, 12 + j, 512 * tc:512 * tc + 512]
                    tt(DVE, sl, sl, SG[:, r, :], ALU.mult, [bMT[12 + j], bSG[r]], [bMT[12 + j]])
            k.soft_barrier()

    def phase_out(l, g, bg=None):
        for t in range(4):
            W, bW = wnext(("wo", l, g, t))
            for j in range(4):
                dc = 4 * t + j
                for tc in range(2):
                    if bg is not None:
                        next(bg, None)
                    ps, bps = psum("O")
                    for kc in range(16):
                        mm(ps, W[:, kc, 128 * j:128 * j + 128], MT[:, kc, 512 * tc:512 * tc + 512], kc == 0, kc == 15,
                           [bW, bMT[kc]], [bps])
                    sl = XT[:, dc, 512 * tc:512 * tc + 512]
                    stt(sl, ps, MODT[l][:, 32 + dc, g:g + 1], sl, ALU.mult, ALU.add, [bps, bMODT[l], bXT[dc]], [bXT[dc]])
        if bg is not None:
            for _ in bg:
                pass
        k.soft_barrier()

    def phase_final(g):
        with nc.sbuf_tensor(un("RBF"), [128, TOK], F32) as RBh, nc.sbuf_tensor(un("YST"), [128, 3, D], F32) as YSTh:
            RB, YST = RBh[:], YSTh[:]
            bRB = k.buf("rbf")
            bYST = [k.buf("yst"), k.buf("yst"), k.buf("yst")]
            norm_stats(RB, bRB)
            for dc in range(16):
                stt(XT[:, dc, :], XT[:, dc, :], GVT[:, 64 + dc:65 + dc], RB, ALU.mult, ALU.mult, [bXT[dc], bGVT, bRB], [bXT[dc]])
            for i in range(8):
                r = i % 3
                for q4 in range(4):
                    ps, bps = psum("ALL")
                    for j in range(4):
                        dc = 4 * q4 + j
                        tr(ps[:, 128 * j:128 * j + 128], XT[:, dc, 128 * i:128 * i + 128], IDF, [bXT[dc], bCONST], [bps],
                           inc=(j == 3))
                    E = ACT if q4 % 2 == 0 else DVE
                    cp(E, YST[:, r, 512 * q4:512 * q4 + 512], ps, [bps], [bYST[r]])
                k.dma_out(SP, bYST[r], yout[g][128 * i:128 * i + 128, :], YST[:, r, :])
            k.barrier()

    HL_P = sb("HL_P", [128, 32], F32)
    HLT_P = sb("HLT_P", [32, 128], F32)
    bHL_P = k.buf("hl")
    bHLT_P = k.buf("hlt")
    k.dma_keep = None
    deferred = []
    CST = sb("CST", [128, 4], F32)
    bCST = k.buf("cst")
    k.op(DVE, lambda: nc.vector.memset(CST[:, 0:1], EPS), [], [bCST])
    k.op(DVE, lambda: nc.vector.memset(CST[:, 1:2], 1.0), [], [bCST])
    k.op(DVE, lambda: nc.vector.memset(CST[:, 2:4], 0.0), [], [bCST])
    k.op(DVE, lambda: nc.vector.memset(CST[0:64, 2:3], 1.0), [], [bCST])
    k.op(DVE, lambda: nc.vector.memset(CST[64:128, 3:4], 1.0), [], [bCST])
    EPS_AP = CST[:, 0:1]
    ONE_AP = CST[:, 1:2]
    k.barrier()

    phase_ada()
    dump("modt0", MODT[0].rearrange("p a b -> p (a b)"), bMODT[0], [128, 96])
    stop = debug.get("stop") if debug else None

    def checkpoint(name, what):
        if stop == name:
            if what == "mt":
                dump("mt", MT.rearrange("p a b -> p (a b)"), bMT, [128, 16 * TOK], BF16)
            elif what == "xt":
                dump("xt", XT.rearrange("p a b -> p (a b)"), bXT, [128, 16 * TOK], F32)
            elif what == "ht":
                dump("ht", HT.rearrange("p a b -> p (a b)"), bHT, [128, 16 * TOK], BF16)
            raise StopBuild()

    def _run_group(g):
        if g != 0:
            load_x(g)
        for l in range(DEPTH):
            if (g, l) != (0, 0):
                phase_norm(l, g)
            checkpoint(f"norm{g}{l}", "ht")
            phase_attn(l, g)
            checkpoint(f"attn{g}{l}", "mt")
            phase_lru(l, g)
            checkpoint(f"lru{g}{l}", "mt")
            phase_hyena(l, g)
            checkpoint(f"hy{g}{l}", "mt")
            nxt = {(0, 0): (0, 1), (0, 1): (1, 0), (1, 0): (1, 1)}.get((g, l))
            phase_out(l, g, gen_G(*nxt) if nxt else None)
            checkpoint(f"out{g}{l}", "xt")
        phase_final(g)

    try:
        for g in range(2):
            _run_group(g)
    except StopBuild:
        if debug and debug.get("sub"):
            dump("mt", MT.rearrange("p a b -> p (a b)"), bMT, [128, 16 * TOK], BF16)
    k.barrier([SP], recycle=False)
    return nc, dbg_outs


_CONSTS = None


def _consts():
    global _CONSTS
    if _CONSTS is None:
        F0, Fi0 = _dft_tables(LP)
        F1, Fi1 = _dft_tables(LS)
        z0, d0 = _hy_consts(LP)
        z1, d1 = _hy_consts(LS)
        cosT, sinT, RT = _rope_tables()
        _CONSTS = {
            "c_idf": np.eye(128, dtype=np.float32),
            "c_idb": np.eye(128).astype(ml_dtypes.bfloat16),
            "c_rt": RT, "c_cos": cosT, "c_sin": sinT,
            "c_z0": z0, "c_z1": z1, "c_dec0": d0, "c_dec1": d1,
            "c_F0": F0, "c_F1": F1, "c_Fi0": Fi0, "c_Fi1": Fi1,
        }
    return _CONSTS


def make_in_maps(inputs):
    f = lambda a: np.ascontiguousarray(np.asarray(a, dtype=np.float32))
    shared = {n: f(inputs[n]) for n in (
        "norm_g", "w_ada", "b_ada", "w_in", "w_out", "lam_q1", "lam_k1", "lam_q2", "lam_k2", "attn_subln_g",
        "lru_conv_w", "lru_conv_b", "lru_wa", "lru_ba", "lru_wi", "lru_bi", "lru_lam", "hy_conv_w", "hy_conv_b",
        "hy_w1", "hy_b1", "hy_w2", "hy_b2", "hy_w3", "hy_d", "final_g")}
    shared.update(_consts())
    xp = f(inputs["x_prompt"])
    xs = f(inputs["x_sample"])
    ckk = f(inputs["cache_k"])
    cvv = f(inputs["cache_v"])
    stl = f(inputs["state_lru"])
    c = f(inputs["c"])
    cctx = f(inputs["c_ctx"])
    maps = []
    for j in range(NCORES):
        s = j % 2
        m = dict(shared)
        m["xp"] = np.ascontiguousarray(xp[4 * j:4 * j + 4].reshape(TOK, D))
        m["xs"] = np.ascontiguousarray(xs[s])
        m["ck"] = np.ascontiguousarray(ckk[s].reshape(DEPTH, PAST, 1024))
        m["cv"] = np.ascontiguousarray(cvv[s].reshape(DEPTH, PAST, 1024))
        m["st_lru"] = np.ascontiguousarray(stl[s])
        m["cvec"] = np.ascontiguousarray(np.stack([cctx, c[s]], axis=0))
        maps.append(m)
    return maps


def kernel(**inputs):
    nc, _ = build_program()
    maps = make_in_maps(inputs)
    res = run_bass_kernel_spmd(nc, maps, core_ids=list(range(NCORES)))
    R = res.results
    y_prompt = np.concatenate([R[j]["yp"].reshape(4, LP, D) for j in range(NCORES)], axis=0).astype(np.float32)
    y_sample = np.stack([R[0]["ys"], R[1]["ys"]], axis=0).astype(np.float32)
    nk_ = np.concatenate([R[j]["nk"].reshape(4, DEPTH, LP, NH, 128) for j in range(NCORES)], axis=0).astype(np.float32)
    nv_ = np.concatenate([R[j]["nv"].reshape(4, DEPTH, LP, NH, 128) for j in range(NCORES)], axis=0).astype(np.float32)
    nst_ = np.concatenate([R[j]["nst"] for j in range(NCORES)], axis=0).astype(np.float32)
    return (y_prompt, y_sample, nk_, nv_, nst_)
```
